# Optimizing a Trainium2 kernel written in Bass

```python
import math
import jax, jax.numpy as jnp
from jax import lax
import numpy as np

D_MODEL = 1024
BATCH = 4
SEQ = 8192
DEPTH = 1

N_META = 16
MIX_WIDTH = D_MODEL
DIFF_WIDTH = MIX_WIDTH // 2
WIN_WIDTH = MIX_WIDTH - DIFF_WIDTH
DIFF_HEAD_DIM = 64
DIFF_HEADS = DIFF_WIDTH // (2 * DIFF_HEAD_DIM)
WIN_HEAD_DIM = 64
WIN_HEADS = WIN_WIDTH // WIN_HEAD_DIM
WIN_KV_HEADS = 2
WIN_GROUP = WIN_HEADS // WIN_KV_HEADS
WINDOW = 128
BLOCK = 128
D_FF = ((8 * D_MODEL // 3 + 127) // 128) * 128
ROPE_THETA = 10000.0
EPS = 1e-6
NEG = -1e30

A_Q = DIFF_HEADS * 2 * DIFF_HEAD_DIM
A_K = DIFF_HEADS * 2 * DIFF_HEAD_DIM
A_V = DIFF_HEADS * 2 * DIFF_HEAD_DIM
B_Q = WIN_HEADS * WIN_HEAD_DIM
B_K = WIN_KV_HEADS * WIN_HEAD_DIM
B_V = WIN_KV_HEADS * WIN_HEAD_DIM
IN_WIDTH = A_Q + A_K + A_V + B_Q + B_K + B_V

kernel_name = "hymba_diff_window_macaron_encoder"


def rmsnorm(x, g):
    xf = x.astype(jnp.float32)
    y = xf * lax.rsqrt(jnp.mean(xf * xf, axis=-1, keepdims=True) + EPS)
    return (y * g.astype(jnp.float32)).astype(x.dtype)


def swiglu(x, w_gate, w_up, w_down):
    return (jax.nn.silu(x @ w_gate) * (x @ w_up)) @ w_down


def rope_tables(length, dim):
    pos = jnp.arange(length, dtype=jnp.float32)
    inv = ROPE_THETA ** (-jnp.arange(0, dim, 2, dtype=jnp.float32) / dim)
    ang = pos[:, None] * inv[None, :]
    return jnp.cos(ang), jnp.sin(ang)


def apply_rope(x, cos, sin):
    shape = (cos.shape[0],) + (1,) * (x.ndim - 3) + (cos.shape[1],)
    c = cos.reshape(shape).astype(x.dtype)
    s = sin.reshape(shape).astype(x.dtype)
    half = x.shape[-1] // 2
    x1, x2 = x[..., :half], x[..., half:]
    return jnp.concatenate([x1 * c - x2 * s, x2 * c + x1 * s], axis=-1)


def diff_attention(q, k, v, lam, sub_gain, lambda_init):
    b_, l_ = q.shape[0], q.shape[1]
    scale = DIFF_HEAD_DIM ** -0.5

    def attend(qb):
        s = jnp.einsum('bqhcd,bkhcd->bhcqk', qb, k, preferred_element_type=jnp.float32) * scale
        p = jax.nn.softmax(s, axis=-1)
        a = p[:, :, 0] - lam * p[:, :, 1]
        return jnp.einsum('bhqk,bkhe->bqhe', a.astype(v.dtype), v)

    out_meta = attend(q[:, :N_META])
    nb = (l_ - N_META) // BLOCK
    qr = q[:, N_META:].reshape(b_, nb, BLOCK, DIFF_HEADS, 2, DIFF_HEAD_DIM).transpose(1, 0, 2, 3, 4, 5)
    out_real = lax.map(attend, qr)
    out_real = out_real.transpose(1, 0, 2, 3, 4).reshape(b_, nb * BLOCK, DIFF_HEADS, 2 * DIFF_HEAD_DIM)
    o = jnp.concatenate([out_meta, out_real], axis=1)
    o = rmsnorm(o, sub_gain) * (1.0 - lambda_init)
    return o.reshape(b_, l_, DIFF_WIDTH)


def window_attention(q, k, v, sink):
    b_, l_ = q.shape[0], q.shape[1]
    s_len = l_ - N_META
    nb = s_len // BLOCK
    scale = WIN_HEAD_DIM ** -0.5
    sink_f = sink.astype(jnp.float32)[:, :, None, None]

    def softmax_with_sink(s):
        sk = jnp.broadcast_to(sink_f, s.shape[:-1] + (1,))
        return jax.nn.softmax(jnp.concatenate([s, sk], axis=-1), axis=-1)[..., :-1]

    s_m = jnp.einsum('bqhgd,bkhd->bhgqk', q[:, :N_META], k, preferred_element_type=jnp.float32) * scale
    o_m = jnp.einsum('bhgqk,bkhd->bqhgd', softmax_with_sink(s_m).astype(v.dtype), v)

    km, vm = k[:, :N_META], v[:, :N_META]
    qr = q[:, N_META:].reshape(b_, nb, BLOCK, WIN_KV_HEADS, WIN_GROUP, WIN_HEAD_DIM)

    def neighbours(t):
        tb = t.reshape(b_, nb, BLOCK, WIN_KV_HEADS, WIN_HEAD_DIM)
        tp = jnp.pad(tb, ((0, 0), (1, 1), (0, 0), (0, 0), (0, 0)))
        return jnp.concatenate([tp[:, :-2], tp[:, 1:-1], tp[:, 2:]], axis=2)

    kr = neighbours(k[:, N_META:])
    vr = neighbours(v[:, N_META:])
    s_meta = jnp.einsum('bnqhgd,bkhd->bnhgqk', qr, km, preferred_element_type=jnp.float32) * scale
    s_band = jnp.einsum('bnqhgd,bnkhd->bnhgqk', qr, kr, preferred_element_type=jnp.float32) * scale
    qi = jnp.arange(BLOCK)[:, None]
    kj = jnp.arange(3 * BLOCK)[None, :] - BLOCK
    kabs = jnp.arange(nb)[:, None, None] * BLOCK + kj[None]
    valid = (jnp.abs(kj - qi)[None] <= WINDOW) & (kabs >= 0) & (kabs < s_len)
    s_band = jnp.where(valid[None, :, None, None], s_band, NEG)
    p = softmax_with_sink(jnp.concatenate([s_meta, s_band], axis=-1)).astype(v.dtype)
    o_r = (jnp.einsum('bnhgqk,bkhd->bnqhgd', p[..., :N_META], vm)
           + jnp.einsum('bnhgqk,bnkhd->bnqhgd', p[..., N_META:], vr))
    o_r = o_r.reshape(b_, s_len, WIN_KV_HEADS, WIN_GROUP, WIN_HEAD_DIM)
    o = jnp.concatenate([o_m, o_r], axis=1)
    return o.reshape(b_, l_, WIN_WIDTH)


def setup_inputs(seed: int = 0) -> dict:
    key = jax.random.key(seed)
    ks = jax.random.split(key, 24)
    f32 = jnp.float32

    def nrm(k, shape, scale):
        return jax.random.normal(k, shape, f32) * scale

    def gain(k, shape):
        return 1.0 + 0.02 * jax.random.normal(k, shape, f32)

    return {
        "x": nrm(ks[0], (BATCH, SEQ, D_MODEL), 1.0),
        "meta_tokens": nrm(ks[1], (N_META, D_MODEL), 1.0),
        "ffn1_norm": gain(ks[2], (DEPTH, D_MODEL)),
        "ffn1_w_gate": nrm(ks[3], (DEPTH, D_MODEL, D_FF), D_MODEL ** -0.5),
        "ffn1_w_up": nrm(ks[4], (DEPTH, D_MODEL, D_FF), D_MODEL ** -0.5),
        "ffn1_w_down": nrm(ks[5], (DEPTH, D_FF, D_MODEL), D_FF ** -0.5),
        "mix_norm": gain(ks[6], (DEPTH, D_MODEL)),
        "w_in": nrm(ks[7], (DEPTH, D_MODEL, IN_WIDTH), D_MODEL ** -0.5),
        "lambda_q1": nrm(ks[8], (DEPTH, DIFF_HEAD_DIM), 0.1),
        "lambda_k1": nrm(ks[9], (DEPTH, DIFF_HEAD_DIM), 0.1),
        "lambda_q2": nrm(ks[10], (DEPTH, DIFF_HEAD_DIM), 0.1),
        "lambda_k2": nrm(ks[11], (DEPTH, DIFF_HEAD_DIM), 0.1),
        "diff_norm": gain(ks[12], (DEPTH, 2 * DIFF_HEAD_DIM)),
        "win_sink": nrm(ks[13], (DEPTH, WIN_HEADS), 0.5),
        "win_norm": gain(ks[14], (DEPTH, WIN_WIDTH)),
        "w_out": nrm(ks[15], (DEPTH, MIX_WIDTH, D_MODEL), MIX_WIDTH ** -0.5),
        "ffn2_norm": gain(ks[16], (DEPTH, D_MODEL)),
        "ffn2_w_gate": nrm(ks[17], (DEPTH, D_MODEL, D_FF), D_MODEL ** -0.5),
        "ffn2_w_up": nrm(ks[18], (DEPTH, D_MODEL, D_FF), D_MODEL ** -0.5),
        "ffn2_w_down": nrm(ks[19], (DEPTH, D_FF, D_MODEL), D_FF ** -0.5),
        "final_norm": gain(ks[20], (D_MODEL,)),
    }


def reference(x, meta_tokens, ffn1_norm, ffn1_w_gate, ffn1_w_up, ffn1_w_down, mix_norm, w_in,
              lambda_q1, lambda_k1, lambda_q2, lambda_k2, diff_norm, win_sink, win_norm, w_out,
              ffn2_norm, ffn2_w_gate, ffn2_w_up, ffn2_w_down, final_norm):
    b_ = x.shape[0]
    meta = jnp.broadcast_to(meta_tokens.astype(x.dtype)[None], (b_, N_META, D_MODEL))
    h = jnp.concatenate([meta, x], axis=1)
    l_ = h.shape[1]
    cos, sin = rope_tables(l_, DIFF_HEAD_DIM)

    for l in range(DEPTH):
        h = h + 0.5 * swiglu(rmsnorm(h, ffn1_norm[l]), ffn1_w_gate[l], ffn1_w_up[l], ffn1_w_down[l])

        u = rmsnorm(h, mix_norm[l])
        z = u @ w_in[l]
        o0 = 0
        qa = z[..., o0:o0 + A_Q].reshape(b_, l_, DIFF_HEADS, 2, DIFF_HEAD_DIM); o0 += A_Q
        ka = z[..., o0:o0 + A_K].reshape(b_, l_, DIFF_HEADS, 2, DIFF_HEAD_DIM); o0 += A_K
        va = z[..., o0:o0 + A_V].reshape(b_, l_, DIFF_HEADS, 2 * DIFF_HEAD_DIM); o0 += A_V
        qb = z[..., o0:o0 + B_Q].reshape(b_, l_, WIN_KV_HEADS, WIN_GROUP, WIN_HEAD_DIM); o0 += B_Q
        kb = z[..., o0:o0 + B_K].reshape(b_, l_, WIN_KV_HEADS, WIN_HEAD_DIM); o0 += B_K
        vb = z[..., o0:o0 + B_V].reshape(b_, l_, WIN_KV_HEADS, WIN_HEAD_DIM)

        qa = apply_rope(qa, cos, sin)
        ka = apply_rope(ka, cos, sin)
        lambda_init = 0.8 - 0.6 * math.exp(-0.3 * l)
        lam = (jnp.exp(jnp.sum(lambda_q1[l].astype(jnp.float32) * lambda_k1[l].astype(jnp.float32)))
               - jnp.exp(jnp.sum(lambda_q2[l].astype(jnp.float32) * lambda_k2[l].astype(jnp.float32)))
               + lambda_init)
        out_a = diff_attention(qa, ka, va, lam, diff_norm[l], lambda_init)

        qb = apply_rope(qb, cos, sin)
        kb = apply_rope(kb, cos, sin)
        out_b = window_attention(qb, kb, vb, win_sink[l].reshape(WIN_KV_HEADS, WIN_GROUP))
        out_b = rmsnorm(out_b, win_norm[l])

        h = h + jnp.concatenate([out_a, out_b], axis=-1) @ w_out[l]

        h = h + 0.5 * swiglu(rmsnorm(h, ffn2_norm[l]), ffn2_w_gate[l], ffn2_w_up[l], ffn2_w_down[l])

    h = rmsnorm(h, final_norm)
    return h[:, N_META:]
```

```python
import numpy as np
import ml_dtypes
from contextlib import ExitStack
import concourse.bass as bass
import concourse.mybir as mybir
from concourse.bass_utils import run_bass_kernel_spmd

F32 = mybir.dt.float32
BF16 = mybir.dt.bfloat16
AF = mybir.ActivationFunctionType
ALU = mybir.AluOpType
bf = ml_dtypes.bfloat16

D = 1024
DFF = 2816
NFC = 22
N_META = 16
EPS = 1e-6
ENGS = ("pe", "act", "dve", "pool", "sp")


class _Op:
    __slots__ = ("fn", "deps", "signal", "dma_key", "dma_val", "eng", "idx")


class Sched:
    def __init__(self, same_engine_sync=True):
        self.streams = {e: [] for e in ENGS}
        self.buf = {}
        self.dma_cnt = {}
        self.same_engine_sync = same_engine_sync

    def op(self, eng, fn, reads=(), writes=(), dma_key=None, extra_deps=()):
        o = _Op()
        o.fn = fn
        o.eng = eng
        o.signal = False
        o.dma_key = dma_key
        o.dma_val = None
        st = self.streams[eng]
        o.idx = len(st)
        deps = set(extra_deps)
        writes = list(writes)
        if dma_key is not None:
            writes.append(("dmasem", dma_key))
        for k in reads:
            b = self.buf.get(k)
            if b is not None and b[0] is not None:
                deps.add(b[0])
        for k in writes:
            b = self.buf.get(k)
            if b is not None:
                if b[0] is not None:
                    deps.add(b[0])
                for t in b[1]:
                    deps.add(t)
        if dma_key is not None:
            c = self.dma_cnt.get(dma_key, 0) + 1
            self.dma_cnt[dma_key] = c
            o.dma_val = 16 * c
            tok = ("D", dma_key, o.dma_val)
        else:
            tok = ("E", eng, o.idx)
        fdeps = []
        for t in deps:
            if t[0] == "E" and t[1] == eng:
                if (not self.same_engine_sync) or eng in ("pe", "sp"):
                    continue
            fdeps.append(t)
        o.deps = fdeps
        st.append(o)
        for k in reads:
            b = self.buf.setdefault(k, [None, []])
            b[1].append(tok)
        for k in writes:
            self.buf[k] = [tok, []]
        return tok

    def barrier(self):
        toks = []
        for e in ENGS:
            for o in reversed(self.streams[e]):
                if o.dma_key is None:
                    toks.append(("E", e, o.idx))
                    break
        for k, c in self.dma_cnt.items():
            toks.append(("D", k, 16 * c))
        for e in ENGS:
            self.op(e, lambda h: h.nop(), extra_deps=[t for t in toks if not (t[0] == "E" and t[1] == e)])

    def finalize(self):
        for e in ENGS:
            for o in self.streams[e]:
                for t in o.deps:
                    if t[0] == "E":
                        self.streams[t[1]][t[2]].signal = True
        self.ms = {}
        for e in ENGS:
            c = 0
            arr = []
            for o in self.streams[e]:
                if o.signal:
                    c += 1
                arr.append(c)
            self.ms[e] = arr

    def emit_engine(self, e, h, sems, dma_sems):
        waited = {}
        for o in self.streams[e]:
            need = {}
            for t in o.deps:
                if t[0] == "E":
                    key = ("E", t[1])
                    v = self.ms[t[1]][t[2]]
                else:
                    key = ("D", t[1])
                    v = t[2]
                if v > need.get(key, 0):
                    need[key] = v
            for key, v in need.items():
                if waited.get(key, 0) >= v:
                    continue
                waited[key] = v
                if key[0] == "E":
                    h.wait_ge(sems[key[1]], v)
                else:
                    h.wait_ge(dma_sems[key[1]], v)
            ins = o.fn(h)
            if o.dma_key is not None:
                ins.then_inc(dma_sems[o.dma_key], 16)
            elif o.signal:
                ins.then_inc(sems[e], 1)


class Arena:
    def __init__(self, t, nwords):
        self.t = t
        self.n = nwords
        self.off = 0
        self.peak = 0

    def _take(self, nw):
        o = self.off
        self.off += nw
        assert self.off <= self.n, ("SBUF arena overflow", self.off, self.n)
        self.peak = max(self.peak, self.off)
        return o

    @staticmethod
    def _shape(ap, shape):
        if len(shape) == 2:
            return ap
        if len(shape) == 3:
            return ap.rearrange("p (a b) -> p a b", b=shape[2])
        if len(shape) == 4:
            return ap.rearrange("p (a b c) -> p a b c", b=shape[2], c=shape[3])
        raise ValueError

    def f32(self, shape):
        n = int(np.prod(shape[1:]))
        o = self._take(n)
        return self._shape(self.t[0:shape[0], o:o + n], shape)

    def bf16(self, shape):
        n = int(np.prod(shape[1:]))
        nw = (n + 1) // 2
        o = self._take(nw)
        ap = self.t[0:shape[0], o:o + nw].bitcast(BF16)[:, 0:n]
        return self._shape(ap, shape)


class _Stop(Exception):
    pass


def build_program(SEQ, stop=None, debug=False):
    HALF = SEQ // 2
    NOWN_T = HALF // 128
    NT = SEQ // 128 + 1
    NTOK = NT * 128
    NG_OWN = HALF // 512
    NG = SEQ // 512 + 1
    META_T = NT - 1

    nc = bass.Bass("TRN2", target_bir_lowering=False)

    def din(name, shape, dt=F32):
        return nc.dram_tensor(name, list(shape), dt, kind="ExternalInput").ap()

    def dscr(name, shape, dt):
        if debug:
            return nc.dram_tensor(name, list(shape), dt, kind="ExternalOutput").ap()
        return nc.dram_tensor(name, list(shape), dt).ap()

    xs = din("xs", [NTOK, D])
    cos_d = din("cosr", [NTOK, 32])
    sin_d = din("sinr", [NTOK, 32])
    ident_d = din("ident", [128, 128], BF16)
    maskL_d = din("maskL", [128, 512], BF16)
    maskR_d = din("maskR", [128, 512], BF16)
    maskLe_d = din("maskLe", [128, 512], BF16)
    maskRe_d = din("maskRe", [128, 512], BF16)
    w1g = din("w1g", [D, DFF]); w1u = din("w1u", [D, DFF]); w1d = din("w1d", [DFF, D])
    w2g = din("w2g", [D, DFF]); w2u = din("w2u", [D, DFF]); w2d = din("w2d", [DFF, D])
    w_in_d = din("w_in", [D, 2304]); w_out_d = din("w_out", [D, D])
    n1_d = din("n1", [1, D]); nm_d = din("nm", [1, D]); n2_d = din("n2", [1, D]); nf_d = din("nf", [1, D])
    lq1_d = din("lq1", [1, 64]); lk1_d = din("lk1", [1, 64]); lq2_d = din("lq2", [1, 64]); lk2_d = din("lk2", [1, 64])
    dnorm_d = din("dnorm", [1, 128]); sink_d = din("sink", [1, 8]); wnorm_d = din("wnorm", [1, 512])
    y_d = nc.dram_tensor("y", [HALF, D], F32, kind="ExternalOutput").ap()

    h1_d = dscr("h1_s", [HALF, D], F32)
    uT_d = dscr("uT_s", [NG, 128, 4096], BF16)
    KTA_d = dscr("KTA_s", [4, 128, NTOK], BF16)
    VA_d = dscr("VA_s", [4, 128, NT, 130], BF16)
    QTA_d = dscr("QTA_s", [4, 128, HALF], BF16)
    KW_d = dscr("KW_s", [128, NTOK], BF16)
    VW_d = dscr("VW_s", [128, NT, 132], BF16)
    QW_d = dscr("QW_s", [128, NOWN_T, 512], BF16)
    cat_d = dscr("cat_s", [HALF, D], BF16)
    WG_d = [dscr("WG1_s", [6, 128, 4096], BF16), dscr("WG2_s", [6, 128, 4096], BF16)]
    WU_d = [dscr("WU1_s", [6, 128, 4096], BF16), dscr("WU2_s", [6, 128, 4096], BF16)]

    S = Sched(same_engine_sync=True)
    es = ExitStack()
    ARENA_W = 52600
    arena_t = es.enter_context(nc.sbuf_tensor("arena", [128, ARENA_W], F32))
    A = Arena(arena_t, ARENA_W)
    ps = es.enter_context(nc.psum_tensor("ps", [128, 4096], F32))

    def bank(i):
        return ps[:, i * 512:(i + 1) * 512]

    def bank_bf(i):
        return bank(i).bitcast(BF16).rearrange("p (k t) -> p k t", t=128)

    uid = [0]

    def ukey(prefix):
        uid[0] += 1
        return "%s#%d" % (prefix, uid[0])

    def dma(eng, out, in_, reads=(), writes=(), key=None):
        return S.op(eng, lambda h: h.dma_start(out=out, in_=in_), reads=reads, writes=writes, dma_key=key)

    ident = A.bf16([128, 128])
    mhalf = A.f32([128, 8])
    neg_lam = A.f32([128, 1])
    lamtmp = A.f32([128, 8])
    lvec = A.f32([128, 4, 64])
    dgain = A.f32([128, 128])
    wgain = A.f32([128, 512])
    esink = A.f32([128, 8])
    masks = A.bf16([128, 4, 512])
    gA = A.f32([128, D])
    gB = A.f32([128, D])
    wd_res = A.bf16([128, NFC, D])
    stg = [A.f32([128, 2048]) for _ in range(2)]
    stb = [A.bf16([128, 2048]) for _ in range(2)]
    PERSIST_NOPREP = None
    P_MARK = A.off

    dma("sp", ident, ident_d, writes=["ident"], key="ident")
    for i, m in enumerate((maskL_d, maskR_d, maskLe_d, maskRe_d)):
        dma("sp", masks[:, i, :], m, writes=["mask%d" % i], key="mask%d" % i)
    S.op("pool", lambda h: h.memset(mhalf, -0.5), writes=["mhalf"])
    for i, v in enumerate((lq1_d, lk1_d, lq2_d, lk2_d)):
        dma("sp", lvec[:, i, :], v[0:1, :].partition_broadcast(128), writes=["lvec%d" % i], key="lvec%d" % i)
    dma("sp", dgain, dnorm_d[0:1, :].partition_broadcast(128), writes=["dgain"], key="dgain")
    dma("sp", wgain, wnorm_d[0:1, :].partition_broadcast(128), writes=["wgain"], key="wgain")
    dma("sp", esink, sink_d[0:1, :].partition_broadcast(128), writes=["esink"], key="esink")
    S.op("dve", lambda h: h.scalar_tensor_tensor(out=lvec[:, 0, :], in0=lvec[:, 0, :], scalar=1.0, in1=lvec[:, 1, :], op0=ALU.mult, op1=ALU.mult, accum_out=lamtmp[:, 0:1]),
         reads=["lvec0", "lvec1"], writes=["lam_d1"])
    S.op("dve", lambda h: h.scalar_tensor_tensor(out=lvec[:, 2, :], in0=lvec[:, 2, :], scalar=1.0, in1=lvec[:, 3, :], op0=ALU.mult, op1=ALU.mult, accum_out=lamtmp[:, 1:2]),
         reads=["lvec2", "lvec3"], writes=["lam_d2"])
    S.op("dve", lambda h: h.tensor_copy(out=lamtmp[:, 4:6], in_=lamtmp[:, 0:2]), reads=["lam_d1", "lam_d2"], writes=["lam_dc"])
    S.op("act", lambda h: h.activation(out=lamtmp[:, 2:4], in_=lamtmp[:, 4:6], func=AF.Exp), reads=["lam_dc"], writes=["lam_e"])
    S.op("act", lambda h: h.activation(out=esink, in_=esink, func=AF.Exp), reads=["esink"], writes=["esink"])
    S.op("dve", lambda h: h.tensor_tensor(out=neg_lam, in0=lamtmp[:, 3:4], in1=lamtmp[:, 2:3], op=ALU.subtract), reads=["lam_e"], writes=["neg_lam0"])
    S.op("dve", lambda h: h.tensor_scalar(out=neg_lam, in0=neg_lam, scalar1=-0.2, scalar2=None, op0=ALU.add), reads=["neg_lam0"], writes=["neg_lam"])
    S.op("dve", lambda h: h.tensor_scalar(out=dgain, in0=dgain, scalar1=0.8, scalar2=None, op0=ALU.mult), reads=["dgain"], writes=["dgain"])

    prep_cnt = [0]

    def prep_items(k, Wg, Wu, Wd, load_eng, cast_engs, store_eng):
        items = []
        Wgv = Wg.rearrange("(k p) n -> p k n", p=128)
        Wuv = Wu.rearrange("(k p) n -> p k n", p=128)
        Wdv = Wd.rearrange("(f p) n -> p f n", p=128)

        def item_gu(src, dst_d, c0, w, chs, j):
            st_ = {}

            def recA():
                i = prep_cnt[0]
                prep_cnt[0] += 1
                st_["s"] = i % 2
                st_["ce"] = cast_engs[i % len(cast_engs)]
                s = st_["s"]
                sv = stg[s][:, 0:2 * w].rearrange("p (k n) -> p k n", n=w)
                dma(load_eng, sv, src[:, 2 * j:2 * j + 2, c0:c0 + w], writes=["stg%d" % s], key="stg%d" % s)

            def recB():
                s, ce = st_["s"], st_["ce"]
                if ce == "act":
                    S.op("act", lambda h: h.activation(out=stb[s][:, 0:2 * w], in_=stg[s][:, 0:2 * w], func=AF.Copy), reads=["stg%d" % s], writes=["stb%d" % s])
                else:
                    S.op(ce, lambda h: h.tensor_copy(out=stb[s][:, 0:2 * w], in_=stg[s][:, 0:2 * w]), reads=["stg%d" % s], writes=["stb%d" % s])
                bv = stb[s][:, 0:2 * w].rearrange("p (k n) -> p k n", n=w)
                off = 0
                for idx, ch in enumerate(chs):
                    cw = 512 if ch < 5 else 256
                    dv = dst_d[ch, :, 0:8 * cw].rearrange("p (k n) -> p k n", n=cw)[:, 2 * j:2 * j + 2, :]
                    dma(store_eng, dv, bv[:, :, off:off + cw], reads=["stb%d" % s], writes=[("wscr", k, id(dst_d), ch)], key="stbst%d_%d" % (s, idx))
                    off += cw
            return (recA, recB)

        def item_d(f0):
            st_ = {}

            def recA():
                i = prep_cnt[0]
                prep_cnt[0] += 1
                st_["s"] = i % 2
                st_["ce"] = cast_engs[i % len(cast_engs)]
                s = st_["s"]
                sv = stg[s].rearrange("p (f n) -> p f n", n=D)
                dma(load_eng, sv, Wdv[:, f0:f0 + 2, :], writes=["stg%d" % s], key="stg%d" % s)

            def recB():
                s, ce = st_["s"], st_["ce"]
                sv = stg[s].rearrange("p (f n) -> p f n", n=D)
                if ce == "act":
                    S.op("act", lambda h: h.activation(out=wd_res[:, f0:f0 + 2, :], in_=sv, func=AF.Copy), reads=["stg%d" % s], writes=["wd%d" % f0])
                else:
                    S.op(ce, lambda h: h.tensor_copy(out=wd_res[:, f0:f0 + 2, :], in_=sv), reads=["stg%d" % s], writes=["wd%d" % f0])
            return (recA, recB)

        for (c0, w, chs) in [(0, 1024, [0, 1]), (1024, 1024, [2, 3]), (2048, 768, [4, 5])]:
            for j in range(4):
                items.append(item_gu(Wgv, WG_d[k], c0, w, chs, j))
                items.append(item_gu(Wuv, WU_d[k], c0, w, chs, j))
        ditems = [item_d(f0) for f0 in range(0, NFC, 2)]
        return items, ditems

    class PrepSeq:
        def __init__(self, items):
            self.items = items
            self.pos = 0

        def step(self, n=1):
            for _ in range(n):
                if self.pos >= len(self.items):
                    return
                if self.pos == 0:
                    self.items[0][0]()
                if self.pos + 1 < len(self.items):
                    self.items[self.pos + 1][0]()
                self.items[self.pos][1]()
                self.pos += 1

    cnt = {"w": 0, "tp": 0, "y": 0, "st": 0}

    jcnt = [0]

    def stats_rstd(srcs, src_keys, n_el, st_tile, key, jks):
        n = len(srcs)
        for j, (sap, sk) in enumerate(zip(srcs, src_keys)):
            ji = jcnt[0] % len(jks)
            jcnt[0] += 1
            jk = jks[ji]
            S.op("dve", lambda h, sap=sap, j=j, jk=jk: h.scalar_tensor_tensor(out=jk[:, 0:n_el], in0=sap, scalar=1.0, in1=sap, op0=ALU.mult, op1=ALU.mult, accum_out=st_tile[:, j:j + 1]),
                 reads=[sk], writes=[key + "_ss%d" % j, "junk%d" % ji])
        S.op("dve", lambda h: h.tensor_scalar(out=st_tile[:, 8:8 + n], in0=st_tile[:, 0:n], scalar1=1.0 / n_el, scalar2=EPS, op0=ALU.mult, op1=ALU.add),
             reads=[key + "_ss%d" % j for j in range(n)], writes=[key + "_v"])
        S.op("pool", lambda h: h.tensor_tensor(out=st_tile[:, 16:16 + n], in0=st_tile[:, 8:8 + n], in1=mhalf[:, 0:n], op=ALU.pow),
             reads=[key + "_v", "mhalf"], writes=[key + "_r"])
        return key + "_r"

    def transposes_to(src_bf, src_key, dstT, dst_key, nblk=8):
        tb = 6 + cnt["tp"] % 2
        cnt["tp"] += 1
        tpv = bank_bf(tb)
        for k in range(nblk):
            S.op("pe", lambda h, k=k: h.transpose(out=tpv[:, k, :], in_=src_bf[:, k * 128:(k + 1) * 128], identity=ident),
                 reads=[src_key, "ident"], writes=["bank%d" % tb])
        S.op("act", lambda h: h.activation(out=dstT, in_=tpv[:, 0:nblk, :], func=AF.Copy), reads=["bank%d" % tb], writes=[dst_key])

    def ffn_gate_up(k, T, xnT, xnT_key, hT, wring, chunks, sg):
        for ch in chunks:
            cw = 512 if ch < 5 else 256
            rs = cnt["w"] % 2
            cnt["w"] += 1
            wg, wu = wring[rs]
            dma("sp", wg[:, 0:8 * cw], WG_d[k][ch, :, 0:8 * cw], reads=[("wscr", k, id(WG_d[k]), ch)], writes=["wg%d" % rs], key="wg%d" % rs)
            dma("sp", wu[:, 0:8 * cw], WU_d[k][ch, :, 0:8 * cw], reads=[("wscr", k, id(WU_d[k]), ch)], writes=["wu%d" % rs], key="wu%d" % rs)
            wgv = wg[:, 0:8 * cw].rearrange("p (k n) -> p k n", n=cw)
            wuv = wu[:, 0:8 * cw].rearrange("p (k n) -> p k n", n=cw)
            for j in range(cw // 128):
                fc = ch * 4 + j
                gb, ub = fc % 2, 2 + fc % 2
                for kc in range(8):
                    S.op("pe", lambda h, kc=kc, j=j, gb=gb, wgv=wgv: h.matmul(bank(gb)[:, 0:T], lhsT=wgv[:, kc, j * 128:(j + 1) * 128], rhs=xnT[:, kc, 0:T], start=(kc == 0), stop=(kc == 7)),
                         reads=[xnT_key, "wg%d" % rs], writes=["bank%d" % gb])
                for kc in range(8):
                    S.op("pe", lambda h, kc=kc, j=j, ub=ub, wuv=wuv: h.matmul(bank(ub)[:, 0:T], lhsT=wuv[:, kc, j * 128:(j + 1) * 128], rhs=xnT[:, kc, 0:T], start=(kc == 0), stop=(kc == 7)),
                         reads=[xnT_key, "wu%d" % rs], writes=["bank%d" % ub])
                sgs = sg[fc % 2]
                S.op("act", lambda h, gb=gb, sgs=sgs: h.activation(out=sgs[:, 0:T], in_=bank(gb)[:, 0:T], func=AF.Silu), reads=["bank%d" % gb], writes=["sg%d" % (fc % 2)])
                S.op("dve", lambda h, ub=ub, sgs=sgs, fc=fc: h.tensor_tensor(out=hT[:, fc, 0:T], in0=bank(ub)[:, 0:T], in1=sgs[:, 0:T], op=ALU.mult),
                     reads=["bank%d" % ub, "sg%d" % (fc % 2)], writes=["hT%d" % fc])

    def ffn_down(nsub, hT, xt, base):
        for sub in range(nsub):
            for dh in range(2):
                yb = 4 + cnt["y"] % 2
                cnt["y"] += 1
                for fc in range(NFC):
                    S.op("pe", lambda h, fc=fc, sub=sub, dh=dh, yb=yb: h.matmul(bank(yb), lhsT=hT[:, fc, sub * 128:(sub + 1) * 128], rhs=wd_res[:, fc, dh * 512:(dh + 1) * 512], start=(fc == 0), stop=(fc == NFC - 1)),
                         reads=["hT%d" % fc, "wd%d" % (fc - fc % 2)], writes=["bank%d" % yb])
                xs_ = xt[:, base + sub, dh * 512:(dh + 1) * 512]
                S.op("dve", lambda h, yb=yb, xs_=xs_: h.scalar_tensor_tensor(out=xs_, in0=bank(yb), scalar=0.5, in1=xs_, op0=ALU.mult, op1=ALU.add),
                     reads=["bank%d" % yb, "xt%d" % (base + sub)], writes=["xt%d" % (base + sub)])

    try:
        items1, ditems1 = prep_items(0, w1g, w1u, w1d, "pool", ["dve", "act"], "pool")
        seq1 = PrepSeq(items1[:16] + ditems1[:4] + items1[16:] + ditems1[4:])
        dma("sp", gA, n1_d[0:1, :].partition_broadcast(128), writes=["gA"], key="gA")
        dma("sp", gB, nm_d[0:1, :].partition_broadcast(128), writes=["gB"], key="gB")

        wring = [(A.bf16([128, 4096]), A.bf16([128, 4096])) for _ in range(2)]
        xt = A.f32([128, 8, D])
        xn = [A.bf16([128, D]) for _ in range(2)]
        un = [A.bf16([128, D]) for _ in range(2)]
        xnT = A.bf16([128, 8, 512])
        uT = A.bf16([128, 8, 512])
        hT = A.bf16([128, NFC, 512])
        sg = [A.f32([128, 512]) for _ in range(2)]
        junk = [A.bf16([128, D]) for _ in range(3)]
        stt = [A.f32([128, 24]) for _ in range(4)]
        A1_PEAK = A.off

        xs_v = xs.rearrange("(t p) d -> p t d", p=128)
        h1_v = h1_d.rearrange("(t p) d -> p t d", p=128)

        def grp(g):
            nsub = 4 if g < NG - 1 else 1
            return nsub, nsub * 128, (g % 2) * 4

        def a1_load(g):
            nsub, T, base = grp(g)
            dma("sp", xt[:, base:base + nsub, :], xs_v[:, 4 * g:4 * g + nsub, :], writes=["xt%d" % (base + s) for s in range(nsub)], key="xtl%d" % (g % 2))

        def a1_pre(g):
            nsub, T, base = grp(g)
            st = stt[cnt["st"] % 4]
            cnt["st"] += 1
            skey = ukey("st")
            rk = stats_rstd([xt[:, base + s, :] for s in range(nsub)], ["xt%d" % (base + s) for s in range(nsub)], D, st, skey, junk)
            for sub in range(nsub):
                xb = xn[sub % 2]
                S.op("dve", lambda h, sub=sub, xb=xb, st=st, xt_=xt, base=base: h.scalar_tensor_tensor(out=xb, in0=xt_[:, base + sub, :], scalar=st[:, 16 + sub:17 + sub], in1=gA, op0=ALU.mult, op1=ALU.mult),
                     reads=["xt%d" % (base + sub), rk, "gA"], writes=["xn%d" % (sub % 2)])
                transposes_to(xb, "xn%d" % (sub % 2), xnT[:, :, sub * 128:(sub + 1) * 128], "xnT")

        def a1_post_a(g):
            nsub, T, base = grp(g)
            if g < NG_OWN:
                dma("pool", h1_v[:, 4 * g:4 * g + 4, :], xt[:, base:base + 4, :], reads=["xt%d" % (base + s) for s in range(4)], key="h1st%d" % (g % 2))
            st = stt[cnt["st"] % 4]
            cnt["st"] += 1
            skey = ukey("st")
            rk = stats_rstd([xt[:, base + s, :] for s in range(nsub)], ["xt%d" % (base + s) for s in range(nsub)], D, st, skey, junk)
            return st, rk

        def a1_post_b(g, st, rk):
            nsub, T, base = grp(g)
            for sub in range(nsub):
                ub_ = un[sub % 2]
                S.op("dve", lambda h, sub=sub, ub_=ub_, xt_=xt, base=base, st=st: h.scalar_tensor_tensor(out=ub_, in0=xt_[:, base + sub, :], scalar=st[:, 16 + sub:17 + sub], in1=gB, op0=ALU.mult, op1=ALU.mult),
                     reads=["xt%d" % (base + sub), rk, "gB"], writes=["un%d" % (sub % 2)])
                transposes_to(ub_, "un%d" % (sub % 2), uT[:, :, sub * 128:(sub + 1) * 128], "uT")
            dv = uT_d[g, :, 0:8 * T].rearrange("p (k t) -> p k t", t=T)
            dma("pool", dv, uT[:, :, 0:T], reads=["uT"], writes=[("uTd", g)], key="uTst")

        seq1.step(8)
        a1_load(0)
        a1_pre(0)
        seq1.step(len(seq1.items))
        pend = None
        for g in range(NG):
            nsub, T, base = grp(g)
            ffn_gate_up(0, T, xnT, "xnT", hT, wring, [0], sg)
            if pend is not None:
                a1_post_b(*pend)
            if g + 1 < NG:
                a1_load(g + 1)
            ffn_gate_up(0, T, xnT, "xnT", hT, wring, [1, 2, 3, 4, 5], sg)
            if g + 1 < NG:
                a1_pre(g + 1)
            ffn_down(nsub, hT, xt, base)
            st, rk = a1_post_a(g)
            pend = (g, st, rk)
        a1_post_b(*pend)

        if stop == "A1":
            raise _Stop()
        S.barrier()
        A.off = P_MARK
        w_in_sb = A.bf16([128, 8, 2304])
        uT2 = [A.bf16([128, 8, 512]) for _ in range(2)]
        cs_t = [(A.f32([128, 4, 32]), A.f32([128, 4, 32])) for _ in range(2)]
        rt = [A.f32([128, 256]) for _ in range(4)]
        ochunk = [A.bf16([128, 512]) for _ in range(3)]
        vst = [A.bf16([128, 4, 130]) for _ in range(2)]
        vwst = [A.bf16([128, 132]) for _ in range(2)]
        kst = [A.bf16([128, 4, 512]) for _ in range(2)]
        qst = [A.bf16([128, 4, 512]) for _ in range(2)]
        kwst = [A.bf16([128, 512]) for _ in range(2)]
        qwst = [A.bf16([128, 4, 512]) for _ in range(2)]
        A2_PEAK = A.off

        w_in_v = w_in_d.rearrange("(k p) n -> p k n", p=128)
        for kc in range(8):
            for hh in range(2):
                dma("pool", w_in_sb[:, kc, hh * 1152:(hh + 1) * 1152], w_in_v[:, kc, hh * 1152:(hh + 1) * 1152], writes=[("w_in", kc, hh)], key="w_in%d_%d" % (kc, hh))
        for s in range(2):
            S.op("pool", lambda h, s=s: h.memset(vst[s][:, :, 128:130], 1.0), writes=["vst%d" % s])
            S.op("pool", lambda h, s=s: h.memset(vwst[s], 1.0), writes=["vwst%d" % s])

        cos_v = cos_d.rearrange("(t p) f -> p t f", p=128)
        sin_v = sin_d.rearrange("(t p) f -> p t f", p=128)
        c2 = {"z": 0, "rt": 0, "oc": 0, "tp": 0, "v": 0}
        WIN_KEYS = ["w_in%d" % kc for kc in range(8)]

        def zproj(uTt, uT_key, sub, c0, cw):
            zb = c2["z"] % 5
            c2["z"] += 1
            for kc in range(8):
                S.op("pe", lambda h, kc=kc, zb=zb: h.matmul(bank(zb)[:, 0:cw], lhsT=uTt[:, kc, sub * 128:(sub + 1) * 128], rhs=w_in_sb[:, kc, c0:c0 + cw], start=(kc == 0), stop=(kc == 7)),
                     reads=[uT_key, ("w_in", kc, 0), ("w_in", kc, 1)], writes=["bank%d" % zb])
            return zb

        def rope(zb, U, cs, cs_keys, sub):
            oc_i = c2["oc"] % 3
            c2["oc"] += 1
            oc = ochunk[oc_i]
            okey = "oc%d" % oc_i
            zv = bank(zb)[:, 0:U * 64].rearrange("p (u two f) -> p u two f", two=2, f=32)
            ov = oc[:, 0:U * 64].rearrange("p (u two f) -> p u two f", two=2, f=32)
            cb = cs[0][:, sub, :].unsqueeze(1).broadcast_to([128, U, 32])
            sb_ = cs[1][:, sub, :].unsqueeze(1).broadcast_to([128, U, 32])
            for half in range(2):
                ta = rt[c2["rt"] % 4]; ka = "rt%d" % (c2["rt"] % 4); c2["rt"] += 1
                tb_ = rt[c2["rt"] % 4]; kb = "rt%d" % (c2["rt"] % 4); c2["rt"] += 1
                tav = ta[:, 0:U * 32].rearrange("p (u f) -> p u f", f=32)
                tbv = tb_[:, 0:U * 32].rearrange("p (u f) -> p u f", f=32)
                S.op("dve", lambda h, tav=tav, half=half: h.tensor_tensor(out=tav, in0=zv[:, :, half, :], in1=cb, op=ALU.mult), reads=["bank%d" % zb] + cs_keys, writes=[ka])
                S.op("dve", lambda h, tbv=tbv, half=half: h.tensor_tensor(out=tbv, in0=zv[:, :, 1 - half, :], in1=sb_, op=ALU.mult), reads=["bank%d" % zb] + cs_keys, writes=[kb])
                op = ALU.subtract if half == 0 else ALU.add
                S.op("dve", lambda h, tav=tav, tbv=tbv, half=half, op=op: h.tensor_tensor(out=ov[:, :, half, :], in0=tav, in1=tbv, op=op), reads=[ka, kb], writes=[okey + "_%d" % half])
            return oc, [okey + "_0", okey + "_1"]

        def tr_blocks(src, src_keys, nblk, dst, dst_key):
            tb = 5 + c2["tp"] % 3
            c2["tp"] += 1
            tpv = bank_bf(tb)
            for k in range(nblk):
                S.op("pe", lambda h, k=k: h.transpose(out=tpv[:, k, :], in_=src[:, k * 128:(k + 1) * 128], identity=ident), reads=list(src_keys) + ["ident"], writes=["bank%d" % tb])
            S.op("act", lambda h: h.activation(out=dst, in_=tpv[:, 0:nblk, :], func=AF.Copy), reads=["bank%d" % tb], writes=[dst_key])

        pendq = []

        def defer(fn):
            pendq.append(fn)
            while len(pendq) > 2:
                pendq.pop(0)()

        def flush():
            while pendq:
                pendq.pop(0)()

        def bq_post(oc, ok, qws_, sub, us):
            tb = 5 + c2["tp"] % 3
            c2["tp"] += 1
            tpv = bank_bf(tb)
            ocv = oc.rearrange("p (g hq d) -> p g hq d", g=2, hq=4)
            for hq in range(4):
                for g2 in range(2):
                    S.op("pe", lambda h, hq=hq, g2=g2: h.transpose(out=tpv[g2 * 64:(g2 + 1) * 64, hq, :], in_=ocv[:, g2, hq, :], identity=ident),
                         reads=list(ok) + ["ident"], writes=["bank%d" % tb])
            S.op("act", lambda h: h.activation(out=qws_[:, sub, :].rearrange("p (hq q) -> p hq q", q=128), in_=tpv[:, 0:4, :], func=AF.Copy),
                 reads=["bank%d" % tb], writes=["qwst%d" % us])

        for g in range(NG):
            nsub, T, _ = grp(g)
            own = g < NG_OWN
            us = g % 2
            uTt = uT2[us]
            dma("sp", uTt[:, :, 0:T], uT_d[g, :, 0:8 * T].rearrange("p (k t) -> p k t", t=T), reads=[("uTd", g)], writes=["uT2_%d" % us], key="uT2_%d" % us)
            cs = cs_t[us]
            dma("sp", cs[0][:, 0:nsub, :], cos_v[:, 4 * g:4 * g + nsub, :], writes=["cos%d" % us], key="cos%d" % us)
            dma("sp", cs[1][:, 0:nsub, :], sin_v[:, 4 * g:4 * g + nsub, :], writes=["sin%d" % us], key="sin%d" % us)
            cs_keys = ["cos%d" % us, "sin%d" % us]
            ks_, qs_, kws_, qws_ = kst[us], qst[us], kwst[us], qwst[us]
            for sub in range(nsub):
                t = 4 * g + sub
                vs_i = c2["v"] % 2
                c2["v"] += 1
                vs_ = vst[vs_i]
                zb = zproj(uTt, "uT2_%d" % us, sub, 1024, 512)
                S.op("act", lambda h, zb=zb, vs_=vs_: h.activation(out=vs_[:, :, 0:128], in_=bank(zb).rearrange("p (h e) -> p h e", e=128), func=AF.Copy),
                     reads=["bank%d" % zb], writes=["vst%d" % vs_i])
                dma("pool", VA_d[:, :, t, :].rearrange("h p e -> p h e"), vs_, reads=["vst%d" % vs_i], writes=[("VAd", t)], key="vast%d" % vs_i)
                zb = zproj(uTt, "uT2_%d" % us, sub, 2048, 256)
                vw_ = vwst[vs_i]
                S.op("act", lambda h, zb=zb, vw_=vw_: h.activation(out=vw_.rearrange("p (g e) -> p g e", e=66)[:, :, 0:64], in_=bank(zb)[:, 128:256].rearrange("p (g e) -> p g e", e=64), func=AF.Copy),
                     reads=["bank%d" % zb], writes=["vwst%d" % vs_i, "bank%d" % zb])
                dma("pool", VW_d[:, t, :], vw_, reads=["vwst%d" % vs_i], writes=[("VWd", t)], key="vwst%d" % vs_i)
                oc, ok = rope(zb, 2, cs, cs_keys, sub)
                defer(lambda oc=oc, ok=ok, sub=sub, kws_=kws_, us=us: tr_blocks(oc, ok, 1, kws_[:, sub * 128:(sub + 1) * 128].rearrange("p (k t) -> p k t", k=1), "kwst%d" % us))
                zb = zproj(uTt, "uT2_%d" % us, sub, 512, 512)
                oc, ok = rope(zb, 8, cs, cs_keys, sub)
                defer(lambda oc=oc, ok=ok, sub=sub, ks_=ks_, us=us: tr_blocks(oc, ok, 4, ks_[:, :, sub * 128:(sub + 1) * 128], "kst%d" % us))
                if own:
                    zb = zproj(uTt, "uT2_%d" % us, sub, 0, 512)
                    oc, ok = rope(zb, 8, cs, cs_keys, sub)
                    defer(lambda oc=oc, ok=ok, sub=sub, qs_=qs_, us=us: tr_blocks(oc, ok, 4, qs_[:, :, sub * 128:(sub + 1) * 128], "qst%d" % us))
                    zb = zproj(uTt, "uT2_%d" % us, sub, 1536, 512)
                    oc, ok = rope(zb, 8, cs, cs_keys, sub)
                    defer(lambda oc=oc, ok=ok, sub=sub, qws_=qws_, us=us: bq_post(oc, ok, qws_, sub, us))
            flush()
            c0t = 512 * g if g < NG - 1 else META_T * 128
            dma("pool", KTA_d[:, :, c0t:c0t + T].rearrange("h p t -> p h t"), ks_[:, :, 0:T], reads=["kst%d" % us], writes=[("KTAd", g)], key="kstst%d" % us)
            dma("pool", KW_d[:, c0t:c0t + T], kws_[:, 0:T], reads=["kwst%d" % us], writes=[("KWd", g)], key="kwstst%d" % us)
            if own:
                dma("pool", QTA_d[:, :, 512 * g:512 * g + 512].rearrange("h p t -> p h t"), qs_, reads=["qst%d" % us], writes=[("QTAd", g)], key="qstst%d" % us)
                dma("pool", QW_d[:, 4 * g:4 * g + 4, :], qws_, reads=["qwst%d" % us], writes=[("QWd", g)], key="qwstst%d" % us)

        if stop == "A2":
            raise _Stop()
        S.barrier()
        A.off = P_MARK
        KVr = [(A.bf16([128, NTOK]), A.bf16([128, NT * 132])) for _ in range(2)]
        QTt = [A.bf16([128, 512]) for _ in range(2)]
        PT = [A.bf16([128, 2, 512]) for _ in range(3)]
        osb = [A.f32([128, 8, 129]) for _ in range(2)]
        otmp = [A.f32([128, 128]) for _ in range(2)]
        ofin = [A.f32([128, 128]) for _ in range(2)]
        cst = [A.bf16([128, 4, 128]) for _ in range(2)]
        est = [A.f32([128, 40]) for _ in range(2)]
        owsb = A.f32([128, 8, 65])
        owf = A.f32([128, 512])
        cwst = [A.bf16([128, 512]) for _ in range(2)]
        junkb = A.f32([128, 512])
        PTw = [A.bf16([128, 512]) for _ in range(16)]
        B_PEAK = A.off

        if debug:
            dbg_v = nc.dram_tensor("dbg_v", [4, 128, NT * 130], BF16, kind="ExternalOutput").ap()
            dbg_osb = nc.dram_tensor("dbg_osb", [4 * (HALF // 512), 128, 8, 129], F32, kind="ExternalOutput").ap()
        items2, ditems2 = prep_items(1, w2g, w2u, w2d, "pool", ["pool"], "pool")
        prep2 = ditems2 + items2
        seq2 = PrepSeq(prep2)

        def prep2_some(n):
            seq2.step(n)

        all_KT_keys = [("KTAd", g) for g in range(NG)]
        all_VA_keys = [("VAd", t) for t in range(NT)]
        all_KW_keys = [("KWd", g) for g in range(NG)]
        all_VW_keys = [("VWd", t) for t in range(NT)]
        cb = {"s": 0, "pt": 0, "u": 0}

        def acc_ap(idx):
            b = 4 + idx // 3
            col = (idx % 3) * 170
            return bank(b)[:, col:col + 129], b

        NQT = HALF // 512
        cat_v = cat_d.rearrange("(t p) d -> p t d", p=128)
        steps = [(hd, qt, kt) for hd in range(4) for qt in range(NQT) for kt in range(NT)]

        def load_kv(hd):
            s = hd % 2
            K_, V_ = KVr[s]
            dma("sp", K_, KTA_d[hd], reads=all_KT_keys, writes=["KTs%d" % s], key="KTl%d" % s)
            dma("sp", V_[:, 0:NT * 130], VA_d[hd].rearrange("p t e -> p (t e)"), reads=all_VA_keys, writes=["Vs%d" % s], key="Vl%d" % s)
            if debug:
                dma("pool", dbg_v[hd], V_[:, 0:NT * 130], reads=["Vs%d" % s], key="dbgv%d" % s)

        def load_q(hd, qt):
            i = (hd * NQT + qt) % 2
            dma("sp", QTt[i], QTA_d[hd, :, qt * 512:(qt + 1) * 512], reads=[("QTAd", qt)], writes=["QT%d" % i], key="QTl%d" % i)

        def rec_S(step):
            hd, qt, kt = step
            s = hd % 2
            K_ = KVr[s][0]
            qi = (hd * NQT + qt) % 2
            kk = 128 if kt < META_T else N_META
            sb_i = cb["s"] % 2
            cb["s"] += 1
            for c in range(2):
                b = 2 * sb_i + c
                S.op("pe", lambda h, c=c, b=b, kt=kt, kk=kk, K_=K_, qi=qi: h.matmul(bank(b)[0:kk, :], lhsT=K_[c * 64:(c + 1) * 64, kt * 128:kt * 128 + kk], rhs=QTt[qi][c * 64:(c + 1) * 64, :], start=True, stop=True),
                     reads=["KTs%d" % s, "QT%d" % qi], writes=["sbank%d" % sb_i])
            pi = cb["pt"] % 3
            cb["pt"] += 1
            S.op("act", lambda h, sb_i=sb_i, pi=pi, kk=kk: h.activation(out=PT[pi][0:kk].rearrange("p c q -> p (c q)"), in_=ps[0:kk, sb_i * 1024:(sb_i + 1) * 1024], func=AF.Exp, scale=0.125),
                 reads=["sbank%d" % sb_i], writes=["PT%d" % pi])
            return pi

        def rec_PV(step, pi):
            hd, qt, kt = step
            s = hd % 2
            V_ = KVr[s][1][:, 0:NT * 130].rearrange("p (t e) -> p t e", e=130)
            kk = 128 if kt < META_T else N_META
            for c in range(2):
                for qc in range(4):
                    idx = c * 4 + qc
                    oap, b = acc_ap(idx)
                    first = (kt == 0 and idx % 3 == 0)
                    S.op("pe", lambda h, c=c, qc=qc, oap=oap, first=first, kk=kk, kt=kt, V_=V_, pi=pi: h.matmul(oap, lhsT=PT[pi][0:kk, c, qc * 128:(qc + 1) * 128], rhs=V_[0:kk, kt, 0:129], start=first, stop=(kt == NT - 1), skip_group_check=True),
                         reads=["PT%d" % pi, "Vs%d" % s], writes=["oacc"])

        def rec_epilogue(hd, qt):
            u = cb["u"] % 2
            cb["u"] += 1
            ob = osb[u]
            e = est[u]
            for idx in range(8):
                oap, b = acc_ap(idx)
                S.op("dve", lambda h, idx=idx, oap=oap: h.tensor_copy(out=ob[:, idx, :], in_=oap), reads=["oacc"], writes=["osb%d_%d" % (u, idx)])
            okeys = ["osb%d_%d" % (u, i) for i in range(8)]
            if debug:
                dma("pool", dbg_osb[hd * NQT + qt], ob, reads=okeys, key="dbgosb%d" % u)
            S.op("dve", lambda h: h.reciprocal(out=e[:, 0:8], in_=ob[:, :, 128]), reads=okeys, writes=["est%d_r" % u])
            S.op("dve", lambda h: h.tensor_scalar(out=e[:, 8:12], in0=e[:, 4:8], scalar1=neg_lam, scalar2=None, op0=ALU.mult), reads=["est%d_r" % u, "neg_lam"], writes=["est%d_rl" % u])
            for qc in range(4):
                ot = otmp[qc % 2]
                of = ofin[qc % 2]
                S.op("dve", lambda h, qc=qc, ot=ot: h.tensor_scalar(out=ot, in0=ob[:, qc, 0:128], scalar1=e[:, qc:qc + 1], scalar2=None, op0=ALU.mult),
                     reads=okeys + ["est%d_r" % u], writes=["otmp%d" % (qc % 2)])
                S.op("dve", lambda h, qc=qc, ot=ot, of=of: h.scalar_tensor_tensor(out=of, in0=ob[:, 4 + qc, 0:128], scalar=e[:, 8 + qc:9 + qc], in1=ot, op0=ALU.mult, op1=ALU.add),
                     reads=okeys + ["est%d_rl" % u, "otmp%d" % (qc % 2)], writes=["ofin%d" % (qc % 2)])
                S.op("dve", lambda h, qc=qc, of=of: h.scalar_tensor_tensor(out=junkb[:, qc * 128:(qc + 1) * 128], in0=of, scalar=1.0, in1=of, op0=ALU.mult, op1=ALU.mult, accum_out=e[:, 16 + qc:17 + qc]),
                     reads=["ofin%d" % (qc % 2)], writes=["est%d_ss%d" % (u, qc), "junkb%d" % qc])
                S.op("dve", lambda h, qc=qc: h.tensor_scalar(out=e[:, 24 + qc:25 + qc], in0=e[:, 16 + qc:17 + qc], scalar1=1.0 / 128, scalar2=EPS, op0=ALU.mult, op1=ALU.add),
                     reads=["est%d_ss%d" % (u, qc)], writes=["est%d_v%d" % (u, qc)])
                S.op("pool", lambda h, qc=qc: h.tensor_tensor(out=e[:, 32 + qc:33 + qc], in0=e[:, 24 + qc:25 + qc], in1=mhalf[:, 0:1], op=ALU.pow),
                     reads=["est%d_v%d" % (u, qc), "mhalf"], writes=["est%d_rs%d" % (u, qc)])
                S.op("dve", lambda h, qc=qc, of=of: h.scalar_tensor_tensor(out=cst[u][:, qc, :], in0=of, scalar=e[:, 32 + qc:33 + qc], in1=dgain, op0=ALU.mult, op1=ALU.mult),
                     reads=["ofin%d" % (qc % 2), "est%d_rs%d" % (u, qc), "dgain"], writes=["cst%d_%d" % (u, qc)])
            dma("pool", cat_v[:, qt * 4:qt * 4 + 4, hd * 128:(hd + 1) * 128], cst[u], reads=["cst%d_%d" % (u, qc) for qc in range(4)], writes=[("catd", qt)], key="cstst%d" % u)

        n_prep_per_unit = (len(prep2) + 4 * NQT - 1) // (4 * NQT) + 1
        load_kv(0)
        load_q(0, 0)
        pend_pv = None
        for i, step in enumerate(steps):
            hd, qt, kt = step
            pi = rec_S(step)
            if pend_pv is not None:
                pstep, ppi = pend_pv
                rec_PV(pstep, ppi)
                if pstep[2] == NT - 1:
                    rec_epilogue(pstep[0], pstep[1])
                    prep2_some(n_prep_per_unit)
            if kt == 0:
                nxt = hd * NQT + qt + 1
                if nxt < 4 * NQT:
                    nh, nq = divmod(nxt, NQT)
                    load_q(nh, nq)
                if qt == 0 and hd + 1 < 4:
                    load_kv(hd + 1)
            pend_pv = (step, pi)
        rec_PV(*pend_pv)
        rec_epilogue(pend_pv[0][0], pend_pv[0][1])
        prep2_some(len(prep2))

        if stop == "Bd":
            raise _Stop()
        Kw_, Vw_ = KVr[0]
        dma("sp", Kw_, KW_d, reads=all_KW_keys, writes=["KTs0"], key="KTl0")
        dma("sp", Vw_, VW_d.rearrange("p t e -> p (t e)"), reads=all_VW_keys, writes=["Vs0"], key="Vl0")
        Vwv = Vw_.rearrange("p (t g e) -> p t g e", g=2, e=66)
        S.barrier()
        wc = {"pt": 0, "s": 0}

        def w_front(n):
            qi = n % 2
            dma("sp", QTt[qi], QW_d[:, n, :], reads=[("QWd", n // 4)], writes=["QT%d" % qi], key="QTl%d" % qi)
            left = n - 1 if n > 0 else NT - 2
            right = n + 1
            blocks = [(left, 0 if n > 0 else 2, 128), (n, None, 128), (right, 1 if n < NOWN_T - 1 else 3, 128), (META_T, None, N_META)]
            pts = []
            for bi, (tix, mi, kk) in enumerate(blocks):
                for g2 in range(2):
                    sbk = wc["s"] % 4
                    wc["s"] += 1
                    S.op("pe", lambda h, g2=g2, sbk=sbk, tix=tix, kk=kk, qi=qi: h.matmul(bank(sbk)[0:kk, :], lhsT=Kw_[g2 * 64:(g2 + 1) * 64, tix * 128:tix * 128 + kk], rhs=QTt[qi][g2 * 64:(g2 + 1) * 64, :], start=True, stop=True),
                         reads=["KTs0", "QT%d" % qi], writes=["wsb%d" % sbk])
                    pslot = wc["pt"] % 16
                    wc["pt"] += 1
                    pap = PTw[pslot]
                    pkey = "PTw%d" % pslot
                    S.op("act", lambda h, sbk=sbk, pap=pap, kk=kk: h.activation(out=pap[0:kk, :], in_=bank(sbk)[0:kk, :], func=AF.Exp, scale=0.125),
                         reads=["wsb%d" % sbk], writes=[pkey])
                    if mi is not None:
                        S.op("dve", lambda h, pap=pap, mi=mi: h.tensor_tensor(out=pap, in0=pap, in1=masks[:, mi, :], op=ALU.mult), reads=[pkey, "mask%d" % mi], writes=[pkey])
                    pts.append((pap, pkey, g2, tix, kk, bi))
            return pts

        def w_back(n, pts):
            for (pap, pkey, g2, tix, kk, bi) in pts:
                for hq in range(4):
                    S.op("pe", lambda h, pap=pap, g2=g2, tix=tix, kk=kk, bi=bi, hq=hq: h.matmul(bank(4 + g2)[:, hq * 128:hq * 128 + 65], lhsT=pap[0:kk, hq * 128:(hq + 1) * 128], rhs=Vwv[0:kk, tix, g2, 0:65], start=(bi == 0 and hq == 0), stop=(bi == 3), skip_group_check=True),
                         reads=[pkey, "Vs0"], writes=["woacc%d" % g2])
            for g2 in range(2):
                S.op("dve", lambda h, g2=g2: h.tensor_copy(out=owsb[:, g2 * 4:(g2 + 1) * 4, :], in_=bank(4 + g2).rearrange("p (hq e) -> p hq e", e=128)[:, :, 0:65]),
                     reads=["woacc%d" % g2], writes=["owsb%d" % g2])
            e = est[n % 2]
            S.op("dve", lambda h, e=e: h.tensor_tensor(out=e[:, 0:8], in0=owsb[:, :, 64], in1=esink, op=ALU.add), reads=["owsb0", "owsb1", "esink"], writes=["west_d"])
            S.op("dve", lambda h, e=e: h.reciprocal(out=e[:, 8:16], in_=e[:, 0:8]), reads=["west_d"], writes=["west_r"])
            S.op("dve", lambda h, e=e: h.tensor_tensor(out=owf.rearrange("p (hh d) -> p hh d", d=64), in0=owsb[:, :, 0:64], in1=e[:, 8:16].unsqueeze(2).broadcast_to([128, 8, 64]), op=ALU.mult),
                 reads=["owsb0", "owsb1", "west_r"], writes=["owf"])
            S.op("dve", lambda h, e=e: h.scalar_tensor_tensor(out=junkb, in0=owf, scalar=1.0, in1=owf, op0=ALU.mult, op1=ALU.mult, accum_out=e[:, 16:17]), reads=["owf"], writes=["west_ss"] + ["junkb%d" % i for i in range(4)])
            S.op("dve", lambda h, e=e: h.tensor_scalar(out=e[:, 17:18], in0=e[:, 16:17], scalar1=1.0 / 512, scalar2=EPS, op0=ALU.mult, op1=ALU.add), reads=["west_ss"], writes=["west_v"])
            S.op("pool", lambda h, e=e: h.tensor_tensor(out=e[:, 18:19], in0=e[:, 17:18], in1=mhalf[:, 0:1], op=ALU.pow), reads=["west_v", "mhalf"], writes=["west_rs"])
            cw_ = cwst[n % 2]
            S.op("dve", lambda h, e=e, cw_=cw_: h.scalar_tensor_tensor(out=cw_, in0=owf, scalar=e[:, 18:19], in1=wgain, op0=ALU.mult, op1=ALU.mult), reads=["owf", "west_rs", "wgain"], writes=["cwst%d" % (n % 2)])
            dma("pool", cat_v[:, n, 512:1024], cw_, reads=["cwst%d" % (n % 2)], writes=[("catd_w", n)], key="cwstst%d" % (n % 2))

        pts_next = w_front(0)
        for n in range(NOWN_T):
            pts_cur = pts_next
            if n + 1 < NOWN_T:
                pts_next = w_front(n + 1)
            w_back(n, pts_cur)

        if stop == "B":
            raise _Stop()
        S.barrier()
        A.off = P_MARK - (2 * 2048 + 2 * 1024)
        dma("sp", gA, n2_d[0:1, :].partition_broadcast(128), writes=["gA"], key="gA")
        dma("sp", gB, nf_d[0:1, :].partition_broadcast(128), writes=["gB"], key="gB")
        wringC = [(A.bf16([128, 4096]), A.bf16([128, 4096])) for _ in range(2)]
        w_out_sb = A.bf16([128, 8, D])
        xtC = A.f32([128, 8, D])
        catt = A.bf16([128, 4, D])
        catT = A.bf16([128, 8, 512])
        xnC = [A.bf16([128, D]) for _ in range(2)]
        xnTC = A.bf16([128, 8, 512])
        hTC = A.bf16([128, NFC, 512])
        sgC = [A.f32([128, 512]) for _ in range(2)]
        junkC = [A.bf16([128, D]) for _ in range(3)]
        sttC = [A.f32([128, 24]) for _ in range(4)]
        C_PEAK = A.off

        w_out_v = w_out_d.rearrange("(k p) n -> p k n", p=128)
        for kc in range(8):
            dma("pool", w_out_sb[:, kc, :], w_out_v[:, kc, :], writes=["w_out%d" % kc], key="w_out%d" % kc)
        y_v = y_d.rearrange("(t p) d -> p t d", p=128)

        def c_load(g):
            base = (g % 2) * 4
            dma("sp", xtC[:, base:base + 4, :], h1_v[:, 4 * g:4 * g + 4, :], writes=["xt%d" % (base + s) for s in range(4)], key="xtl%d" % (g % 2))
            dma("sp", catt, cat_v[:, 4 * g:4 * g + 4, :], reads=[("catd", g)] + [("catd_w", 4 * g + s) for s in range(4)], writes=["catt"], key="cattl")

        def c_pre(g):
            base = (g % 2) * 4
            for sub in range(4):
                transposes_to(catt[:, sub, :], "catt", catT[:, :, sub * 128:(sub + 1) * 128], "catT")
            for sub in range(4):
                for dh in range(2):
                    yb = 4 + cnt["y"] % 2
                    cnt["y"] += 1
                    for kc in range(8):
                        S.op("pe", lambda h, kc=kc, sub=sub, dh=dh, yb=yb: h.matmul(bank(yb), lhsT=catT[:, kc, sub * 128:(sub + 1) * 128], rhs=w_out_sb[:, kc, dh * 512:(dh + 1) * 512], start=(kc == 0), stop=(kc == 7)),
                             reads=["catT", "w_out%d" % kc], writes=["bank%d" % yb])
                    xs_ = xtC[:, base + sub, dh * 512:(dh + 1) * 512]
                    S.op("dve", lambda h, yb=yb, xs_=xs_: h.tensor_tensor(out=xs_, in0=bank(yb), in1=xs_, op=ALU.add), reads=["bank%d" % yb, "xt%d" % (base + sub)], writes=["xt%d" % (base + sub)])
            st = sttC[cnt["st"] % 4]
            cnt["st"] += 1
            rk = stats_rstd([xtC[:, base + s, :] for s in range(4)], ["xt%d" % (base + s) for s in range(4)], D, st, ukey("st"), junkC)
            for sub in range(4):
                xb = xnC[sub % 2]
                S.op("dve", lambda h, sub=sub, xb=xb, st=st: h.scalar_tensor_tensor(out=xb, in0=xtC[:, base + sub, :], scalar=st[:, 16 + sub:17 + sub], in1=gA, op0=ALU.mult, op1=ALU.mult),
                     reads=["xt%d" % (base + sub), rk, "gA"], writes=["xn%d" % (sub % 2)])
                transposes_to(xb, "xn%d" % (sub % 2), xnTC[:, :, sub * 128:(sub + 1) * 128], "xnT")

        def c_post(g):
            base = (g % 2) * 4
            st = sttC[cnt["st"] % 4]
            cnt["st"] += 1
            rk = stats_rstd([xtC[:, base + s, :] for s in range(4)], ["xt%d" % (base + s) for s in range(4)], D, st, ukey("st"), junkC)
            for sub in range(4):
                S.op("dve", lambda h, sub=sub, st=st: h.scalar_tensor_tensor(out=xtC[:, base + sub, :], in0=xtC[:, base + sub, :], scalar=st[:, 16 + sub:17 + sub], in1=gB, op0=ALU.mult, op1=ALU.mult),
                     reads=["xt%d" % (base + sub), rk, "gB"], writes=["xt%d" % (base + sub)])
            return dma("pool", y_v[:, 4 * g:4 * g + 4, :], xtC[:, base:base + 4, :], reads=["xt%d" % (base + s) for s in range(4)], key="yst%d" % (g % 2))

        c_load(0)
        c_pre(0)
        out_toks = []
        for g in range(NG_OWN):
            base = (g % 2) * 4
            if g + 1 < NG_OWN:
                c_load(g + 1)
            ffn_gate_up(1, 512, xnTC, "xnT", hTC, wringC, [0, 1, 2, 3, 4, 5], sgC)
            if g + 1 < NG_OWN:
                c_pre(g + 1)
            ffn_down(4, hTC, xtC, base)
            out_toks.append(c_post(g))
        S.op("pool", lambda h: h.nop(), extra_deps=out_toks[-2:])
    except _Stop:
        S.barrier()

    print("[kernel] arena peak (words): %d of %d" % (A.peak, ARENA_W))
    print("[kernel] instr counts:", {e: len(S.streams[e]) for e in ENGS})
    S.finalize()
    sems = {e: es.enter_context(nc.semaphore("sem_%s" % e)) for e in ENGS}
    dsems = {k: es.enter_context(nc.semaphore("dsem_%s" % k)) for k in S.dma_cnt}
    print("[kernel] semaphores:", len(sems) + len(dsems))
    with nc.Block() as block:
        @block.tensor
        def _(h):
            S.emit_engine("pe", h, sems, dsems)

        @block.scalar
        def _(h):
            S.emit_engine("act", h, sems, dsems)

        @block.vector
        def _(h):
            S.emit_engine("dve", h, sems, dsems)

        @block.gpsimd
        def _(h):
            S.emit_engine("pool", h, sems, dsems)

        @block.sync
        def _(h):
            S.emit_engine("sp", h, sems, dsems)
    es.close()
    return nc


def _rope_tables(pos):
    inv = np.float32(10000.0) ** (-(np.arange(0, 64, 2, dtype=np.float32)) / np.float32(64))
    ang = (pos.astype(np.float32)[:, None] * inv[None, :]).astype(np.float32)
    return np.cos(ang).astype(np.float32), np.sin(ang).astype(np.float32)


def kernel(x, meta_tokens, ffn1_norm, ffn1_w_gate, ffn1_w_up, ffn1_w_down, mix_norm, w_in,
           lambda_q1, lambda_k1, lambda_q2, lambda_k2, diff_norm, win_sink, win_norm, w_out,
           ffn2_norm, ffn2_w_gate, ffn2_w_up, ffn2_w_down, final_norm):
    x = np.asarray(x, dtype=np.float32)
    B, SEQ, _ = x.shape
    HALF = SEQ // 2
    NT = SEQ // 128 + 1
    NTOK = NT * 128
    nc = build_program(SEQ)
    f32 = lambda a: np.ascontiguousarray(np.asarray(a, dtype=np.float32))
    ident = np.eye(128, dtype=np.float32).astype(bf)
    jj = np.arange(128)[:, None]
    qq = np.arange(128)[None, :]
    mL = np.tile((jj >= qq).astype(np.float32), (1, 4)).astype(bf)
    mR = np.tile((jj <= qq).astype(np.float32), (1, 4)).astype(bf)
    mZ = np.zeros((128, 512), dtype=bf)
    meta_pad = np.zeros((128, 1024), np.float32)
    meta_pad[:N_META] = np.asarray(meta_tokens, np.float32)
    common = {
        "ident": ident, "maskL": mL, "maskR": mR,
        "w1g": f32(ffn1_w_gate[0]), "w1u": f32(ffn1_w_up[0]), "w1d": f32(ffn1_w_down[0]),
        "w2g": f32(ffn2_w_gate[0]), "w2u": f32(ffn2_w_up[0]), "w2d": f32(ffn2_w_down[0]),
        "w_in": f32(w_in[0]), "w_out": f32(w_out[0]),
        "n1": f32(ffn1_norm).reshape(1, -1), "nm": f32(mix_norm).reshape(1, -1), "n2": f32(ffn2_norm).reshape(1, -1), "nf": f32(final_norm).reshape(1, -1),
        "lq1": f32(lambda_q1).reshape(1, -1), "lk1": f32(lambda_k1).reshape(1, -1), "lq2": f32(lambda_q2).reshape(1, -1), "lk2": f32(lambda_k2).reshape(1, -1),
        "dnorm": f32(diff_norm).reshape(1, -1), "sink": f32(win_sink).reshape(1, -1), "wnorm": f32(win_norm).reshape(1, -1),
    }
    in_maps = []
    for c in range(8):
        b, half = divmod(c, 2)
        own = slice(half * HALF, (half + 1) * HALF)
        oth = slice((1 - half) * HALF, (2 - half) * HALF)
        xs = np.concatenate([x[b, own], x[b, oth], meta_pad], axis=0)
        pos = np.concatenate([N_META + np.arange(SEQ)[own], N_META + np.arange(SEQ)[oth], np.arange(N_META), np.zeros(128 - N_META, np.int64)])
        cs, sn = _rope_tables(pos)
        m = dict(common)
        m["xs"] = np.ascontiguousarray(xs)
        m["cosr"] = cs
        m["sinr"] = sn
        m["maskLe"] = mZ if half == 0 else mL
        m["maskRe"] = mR if half == 0 else mZ
        in_maps.append(m)
    res = run_bass_kernel_spmd(nc, in_maps, core_ids=list(range(8)))
    out = np.empty((B, SEQ, 1024), np.float32)
    for c in range(8):
        b, half = divmod(c, 2)
        out[b, half * HALF:(half + 1) * HALF] = res.results[c]["y"]
    return out
```

```python
import numpy as np
import ml_dtypes
from contextlib import ExitStack
import concourse.bass as bass
import concourse.mybir as mybir
from concourse.bass_utils import run_bass_kernel_spmd

F32 = mybir.dt.float32
BF16 = mybir.dt.bfloat16
AF = mybir.ActivationFunctionType
ALU = mybir.AluOpType
bf = ml_dtypes.bfloat16

D = 1024
DFF = 2816
NFC = 22
N_META = 16
EPS = 1e-6
ENGS = ("pe", "act", "dve", "pool", "sp")


class _Op:
    __slots__ = ("fn", "deps", "signal", "dma_key", "dma_val", "eng", "idx")


class Sched:
    def __init__(self, same_engine_sync=True):
        self.streams = {e: [] for e in ENGS}
        self.buf = {}
        self.dma_cnt = {}
        self.same_engine_sync = same_engine_sync

    def op(self, eng, fn, reads=(), writes=(), dma_key=None, extra_deps=()):
        o = _Op()
        o.fn = fn
        o.eng = eng
        o.signal = False
        o.dma_key = dma_key
        o.dma_val = None
        st = self.streams[eng]
        o.idx = len(st)
        deps = set(extra_deps)
        writes = list(writes)
        if dma_key is not None:
            writes.append(("dmasem", dma_key))
        for k in reads:
            b = self.buf.get(k)
            if b is not None and b[0] is not None:
                deps.add(b[0])
        for k in writes:
            b = self.buf.get(k)
            if b is not None:
                if b[0] is not None:
                    deps.add(b[0])
                for t in b[1]:
                    deps.add(t)
        if dma_key is not None:
            c = self.dma_cnt.get(dma_key, 0) + 1
            self.dma_cnt[dma_key] = c
            o.dma_val = 16 * c
            tok = ("D", dma_key, o.dma_val)
        else:
            tok = ("E", eng, o.idx)
        fdeps = []
        for t in deps:
            if t[0] == "E" and t[1] == eng:
                if (not self.same_engine_sync) or eng in ("pe", "sp"):
                    continue
            fdeps.append(t)
        o.deps = fdeps
        st.append(o)
        for k in reads:
            b = self.buf.setdefault(k, [None, []])
            b[1].append(tok)
        for k in writes:
            self.buf[k] = [tok, []]
        return tok

    def barrier(self):
        toks = []
        for e in ENGS:
            for o in reversed(self.streams[e]):
                if o.dma_key is None:
                    toks.append(("E", e, o.idx))
                    break
        for k, c in self.dma_cnt.items():
            toks.append(("D", k, 16 * c))
        for e in ENGS:
            self.op(e, lambda h: h.nop(), extra_deps=[t for t in toks if not (t[0] == "E" and t[1] == e)])

    def finalize(self):
        for e in ENGS:
            for o in self.streams[e]:
                for t in o.deps:
                    if t[0] == "E":
                        self.streams[t[1]][t[2]].signal = True
        self.ms = {}
        for e in ENGS:
            c = 0
            arr = []
            for o in self.streams[e]:
                if o.signal:
                    c += 1
                arr.append(c)
            self.ms[e] = arr

    def emit_engine(self, e, h, sems, dma_sems):
        waited = {}
        for o in self.streams[e]:
            need = {}
            for t in o.deps:
                if t[0] == "E":
                    key = ("E", t[1])
                    v = self.ms[t[1]][t[2]]
                else:
                    key = ("D", t[1])
                    v = t[2]
                if v > need.get(key, 0):
                    need[key] = v
            for key, v in need.items():
                if waited.get(key, 0) >= v:
                    continue
                waited[key] = v
                if key[0] == "E":
                    h.wait_ge(sems[key[1]], v)
                else:
                    h.wait_ge(dma_sems[key[1]], v)
            ins = o.fn(h)
            if o.dma_key is not None:
                ins.then_inc(dma_sems[o.dma_key], 16)
            elif o.signal:
                ins.then_inc(sems[e], 1)


class Arena:
    def __init__(self, t, nwords):
        self.t = t
        self.n = nwords
        self.off = 0
        self.peak = 0

    def _take(self, nw):
        o = self.off
        self.off += nw
        assert self.off <= self.n, ("SBUF arena overflow", self.off, self.n)
        self.peak = max(self.peak, self.off)
        return o

    @staticmethod
    def _shape(ap, shape):
        if len(shape) == 2:
            return ap
        if len(shape) == 3:
            return ap.rearrange("p (a b) -> p a b", b=shape[2])
        if len(shape) == 4:
            return ap.rearrange("p (a b c) -> p a b c", b=shape[2], c=shape[3])
        raise ValueError

    def f32(self, shape):
        n = int(np.prod(shape[1:]))
        o = self._take(n)
        return self._shape(self.t[0:shape[0], o:o + n], shape)

    def bf16(self, shape):
        n = int(np.prod(shape[1:]))
        nw = (n + 1) // 2
        o = self._take(nw)
        ap = self.t[0:shape[0], o:o + nw].bitcast(BF16)[:, 0:n]
        return self._shape(ap, shape)


class _Stop(Exception):
    pass


def build_program(SEQ, stop=None, debug=False):
    HALF = SEQ // 2
    NOWN_T = HALF // 128
    NT = SEQ // 128 + 1
    NTOK = NT * 128
    NG_OWN = HALF // 512
    NG = SEQ // 512 + 1
    META_T = NT - 1

    nc = bass.Bass("TRN2", target_bir_lowering=False)

    def din(name, shape, dt=F32):
        return nc.dram_tensor(name, list(shape), dt, kind="ExternalInput").ap()

    def dscr(name, shape, dt):
        if debug:
            return nc.dram_tensor(name, list(shape), dt, kind="ExternalOutput").ap()
        return nc.dram_tensor(name, list(shape), dt).ap()

    xs = din("xs", [NTOK, D])
    cos_d = din("cosr", [NTOK, 32])
    sin_d = din("sinr", [NTOK, 32])
    ident_d = din("ident", [128, 128], BF16)
    maskL_d = din("maskL", [128, 512], BF16)
    maskR_d = din("maskR", [128, 512], BF16)
    maskLe_d = din("maskLe", [128, 512], BF16)
    maskRe_d = din("maskRe", [128, 512], BF16)
    w1g = din("w1g", [D, DFF]); w1u = din("w1u", [D, DFF]); w1d = din("w1d", [DFF, D])
    w2g = din("w2g", [D, DFF]); w2u = din("w2u", [D, DFF]); w2d = din("w2d", [DFF, D])
    w_in_d = din("w_in", [D, 2304]); w_out_d = din("w_out", [D, D])
    n1_d = din("n1", [1, D]); nm_d = din("nm", [1, D]); n2_d = din("n2", [1, D]); nf_d = din("nf", [1, D])
    lq1_d = din("lq1", [1, 64]); lk1_d = din("lk1", [1, 64]); lq2_d = din("lq2", [1, 64]); lk2_d = din("lk2", [1, 64])
    dnorm_d = din("dnorm", [1, 128]); sink_d = din("sink", [1, 8]); wnorm_d = din("wnorm", [1, 512])
    y_d = nc.dram_tensor("y", [HALF, D], F32, kind="ExternalOutput").ap()

    h1_d = dscr("h1_s", [HALF, D], F32)
    uT_d = dscr("uT_s", [NG, 128, 4096], BF16)
    KTA_d = dscr("KTA_s", [4, 128, NTOK], BF16)
    VA_d = dscr("VA_s", [4, 128, NT, 130], BF16)
    QTA_d = dscr("QTA_s", [4, 128, HALF], BF16)
    KW_d = dscr("KW_s", [128, NTOK], BF16)
    VW_d = dscr("VW_s", [128, NT, 132], BF16)
    QW_d = dscr("QW_s", [128, NOWN_T, 512], BF16)
    cat_d = dscr("cat_s", [HALF, D], BF16)
    WG_d = [dscr("WG1_s", [6, 128, 4096], BF16), dscr("WG2_s", [6, 128, 4096], BF16)]
    WU_d = [dscr("WU1_s", [6, 128, 4096], BF16), dscr("WU2_s", [6, 128, 4096], BF16)]

    S = Sched(same_engine_sync=True)
    es = ExitStack()
    ARENA_W = 52600
    arena_t = es.enter_context(nc.sbuf_tensor("arena", [128, ARENA_W], F32))
    A = Arena(arena_t, ARENA_W)
    ps = es.enter_context(nc.psum_tensor("ps", [128, 4096], F32))

    def bank(i):
        return ps[:, i * 512:(i + 1) * 512]

    def bank_bf(i):
        return bank(i).bitcast(BF16).rearrange("p (k t) -> p k t", t=128)

    uid = [0]

    def ukey(prefix):
        uid[0] += 1
        return "%s#%d" % (prefix, uid[0])

    def dma(eng, out, in_, reads=(), writes=(), key=None):
        return S.op(eng, lambda h: h.dma_start(out=out, in_=in_), reads=reads, writes=writes, dma_key=key)

    ident = A.bf16([128, 128])
    mhalf = A.f32([128, 8])
    neg_lam = A.f32([128, 1])
    lamtmp = A.f32([128, 8])
    lvec = A.f32([128, 4, 64])
    dgain = A.f32([128, 128])
    wgain = A.f32([128, 512])
    esink = A.f32([128, 8])
    masks = A.bf16([128, 4, 512])
    gA = A.f32([128, D])
    gB = A.f32([128, D])
    wd_res = A.bf16([128, NFC, D])
    stg = [A.f32([128, 2048]) for _ in range(2)]
    stb = [A.bf16([128, 2048]) for _ in range(2)]
    PERSIST_NOPREP = None
    P_MARK = A.off

    dma("sp", ident, ident_d, writes=["ident"], key="ident")
    for i, m in enumerate((maskL_d, maskR_d, maskLe_d, maskRe_d)):
        dma("sp", masks[:, i, :], m, writes=["mask%d" % i], key="mask%d" % i)
    S.op("pool", lambda h: h.memset(mhalf, -0.5), writes=["mhalf"])
    for i, v in enumerate((lq1_d, lk1_d, lq2_d, lk2_d)):
        dma("sp", lvec[:, i, :], v[0:1, :].partition_broadcast(128), writes=["lvec%d" % i], key="lvec%d" % i)
    dma("sp", dgain, dnorm_d[0:1, :].partition_broadcast(128), writes=["dgain"], key="dgain")
    dma("sp", wgain, wnorm_d[0:1, :].partition_broadcast(128), writes=["wgain"], key="wgain")
    dma("sp", esink, sink_d[0:1, :].partition_broadcast(128), writes=["esink"], key="esink")
    S.op("dve", lambda h: h.scalar_tensor_tensor(out=lvec[:, 0, :], in0=lvec[:, 0, :], scalar=1.0, in1=lvec[:, 1, :], op0=ALU.mult, op1=ALU.mult, accum_out=lamtmp[:, 0:1]),
         reads=["lvec0", "lvec1"], writes=["lam_d1"])
    S.op("dve", lambda h: h.scalar_tensor_tensor(out=lvec[:, 2, :], in0=lvec[:, 2, :], scalar=1.0, in1=lvec[:, 3, :], op0=ALU.mult, op1=ALU.mult, accum_out=lamtmp[:, 1:2]),
         reads=["lvec2", "lvec3"], writes=["lam_d2"])
    S.op("dve", lambda h: h.tensor_copy(out=lamtmp[:, 4:6], in_=lamtmp[:, 0:2]), reads=["lam_d1", "lam_d2"], writes=["lam_dc"])
    S.op("act", lambda h: h.activation(out=lamtmp[:, 2:4], in_=lamtmp[:, 4:6], func=AF.Exp), reads=["lam_dc"], writes=["lam_e"])
    S.op("act", lambda h: h.activation(out=esink, in_=esink, func=AF.Exp), reads=["esink"], writes=["esink"])
    S.op("dve", lambda h: h.tensor_tensor(out=neg_lam, in0=lamtmp[:, 3:4], in1=lamtmp[:, 2:3], op=ALU.subtract), reads=["lam_e"], writes=["neg_lam0"])
    S.op("dve", lambda h: h.tensor_scalar(out=neg_lam, in0=neg_lam, scalar1=-0.2, scalar2=None, op0=ALU.add), reads=["neg_lam0"], writes=["neg_lam"])
    S.op("dve", lambda h: h.tensor_scalar(out=dgain, in0=dgain, scalar1=0.8, scalar2=None, op0=ALU.mult), reads=["dgain"], writes=["dgain"])

    prep_cnt = [0]

    def prep_items(k, Wg, Wu, Wd, load_eng, cast_engs, store_eng):
        items = []
        Wgv = Wg.rearrange("(k p) n -> p k n", p=128)
        Wuv = Wu.rearrange("(k p) n -> p k n", p=128)
        Wdv = Wd.rearrange("(f p) n -> p f n", p=128)

        def item_gu(src, dst_d, c0, w, chs, j):
            st_ = {}

            def recA():
                i = prep_cnt[0]
                prep_cnt[0] += 1
                st_["s"] = i % 2
                st_["ce"] = cast_engs[i % len(cast_engs)]
                s = st_["s"]
                sv = stg[s][:, 0:2 * w].rearrange("p (k n) -> p k n", n=w)
                dma(load_eng, sv, src[:, 2 * j:2 * j + 2, c0:c0 + w], writes=["stg%d" % s], key="stg%d" % s)

            def recB():
                s, ce = st_["s"], st_["ce"]
                if ce == "act":
                    S.op("act", lambda h: h.activation(out=stb[s][:, 0:2 * w], in_=stg[s][:, 0:2 * w], func=AF.Copy), reads=["stg%d" % s], writes=["stb%d" % s])
                else:
                    S.op(ce, lambda h: h.tensor_copy(out=stb[s][:, 0:2 * w], in_=stg[s][:, 0:2 * w]), reads=["stg%d" % s], writes=["stb%d" % s])
                bv = stb[s][:, 0:2 * w].rearrange("p (k n) -> p k n", n=w)
                off = 0
                for idx, ch in enumerate(chs):
                    cw = 512 if ch < 5 else 256
                    dv = dst_d[ch, :, 0:8 * cw].rearrange("p (k n) -> p k n", n=cw)[:, 2 * j:2 * j + 2, :]
                    dma(store_eng, dv, bv[:, :, off:off + cw], reads=["stb%d" % s], writes=[("wscr", k, id(dst_d), ch)], key="stbst%d_%d" % (s, idx))
                    off += cw
            return (recA, recB)

        def item_d(f0):
            st_ = {}

            def recA():
                i = prep_cnt[0]
                prep_cnt[0] += 1
                st_["s"] = i % 2
                st_["ce"] = cast_engs[i % len(cast_engs)]
                s = st_["s"]
                sv = stg[s].rearrange("p (f n) -> p f n", n=D)
                dma(load_eng, sv, Wdv[:, f0:f0 + 2, :], writes=["stg%d" % s], key="stg%d" % s)

            def recB():
                s, ce = st_["s"], st_["ce"]
                sv = stg[s].rearrange("p (f n) -> p f n", n=D)
                if ce == "act":
                    S.op("act", lambda h: h.activation(out=wd_res[:, f0:f0 + 2, :], in_=sv, func=AF.Copy), reads=["stg%d" % s], writes=["wd%d" % f0])
                else:
                    S.op(ce, lambda h: h.tensor_copy(out=wd_res[:, f0:f0 + 2, :], in_=sv), reads=["stg%d" % s], writes=["wd%d" % f0])
            return (recA, recB)

        for (c0, w, chs) in [(0, 1024, [0, 1]), (1024, 1024, [2, 3]), (2048, 768, [4, 5])]:
            for j in range(4):
                items.append(item_gu(Wgv, WG_d[k], c0, w, chs, j))
                items.append(item_gu(Wuv, WU_d[k], c0, w, chs, j))
        ditems = [item_d(f0) for f0 in range(0, NFC, 2)]
        return items, ditems

    class PrepSeq:
        def __init__(self, items):
            self.items = items
            self.pos = 0

        def step(self, n=1):
            for _ in range(n):
                if self.pos >= len(self.items):
                    return
                if self.pos == 0:
                    self.items[0][0]()
                if self.pos + 1 < len(self.items):
                    self.items[self.pos + 1][0]()
                self.items[self.pos][1]()
                self.pos += 1

    cnt = {"w": 0, "tp": 0, "y": 0, "st": 0}

    jcnt = [0]

    def stats_rstd(srcs, src_keys, n_el, st_tile, key, jks):
        n = len(srcs)
        for j, (sap, sk) in enumerate(zip(srcs, src_keys)):
            ji = jcnt[0] % len(jks)
            jcnt[0] += 1
            jk = jks[ji]
            S.op("dve", lambda h, sap=sap, j=j, jk=jk: h.scalar_tensor_tensor(out=jk[:, 0:n_el], in0=sap, scalar=1.0, in1=sap, op0=ALU.mult, op1=ALU.mult, accum_out=st_tile[:, j:j + 1]),
                 reads=[sk], writes=[key + "_ss%d" % j, "junk%d" % ji])
        S.op("dve", lambda h: h.tensor_scalar(out=st_tile[:, 8:8 + n], in0=st_tile[:, 0:n], scalar1=1.0 / n_el, scalar2=EPS, op0=ALU.mult, op1=ALU.add),
             reads=[key + "_ss%d" % j for j in range(n)], writes=[key + "_v"])
        S.op("pool", lambda h: h.tensor_tensor(out=st_tile[:, 16:16 + n], in0=st_tile[:, 8:8 + n], in1=mhalf[:, 0:n], op=ALU.pow),
             reads=[key + "_v", "mhalf"], writes=[key + "_r"])
        return key + "_r"

    def transposes_to(src_bf, src_key, dstT, dst_key, nblk=8):
        tb = 6 + cnt["tp"] % 2
        cnt["tp"] += 1
        tpv = bank_bf(tb)
        for k in range(nblk):
            S.op("pe", lambda h, k=k: h.transpose(out=tpv[:, k, :], in_=src_bf[:, k * 128:(k + 1) * 128], identity=ident),
                 reads=[src_key, "ident"], writes=["bank%d" % tb])
        S.op("act", lambda h: h.activation(out=dstT, in_=tpv[:, 0:nblk, :], func=AF.Copy), reads=["bank%d" % tb], writes=[dst_key])

    def ffn_gate_up(k, T, xnT, xnT_key, hT, wring, chunks, sg):
        for ch in chunks:
            cw = 512 if ch < 5 else 256
            rs = cnt["w"] % 2
            cnt["w"] += 1
            wg, wu = wring[rs]
            dma("sp", wg[:, 0:8 * cw], WG_d[k][ch, :, 0:8 * cw], reads=[("wscr", k, id(WG_d[k]), ch)], writes=["wg%d" % rs], key="wg%d" % rs)
            dma("sp", wu[:, 0:8 * cw], WU_d[k][ch, :, 0:8 * cw], reads=[("wscr", k, id(WU_d[k]), ch)], writes=["wu%d" % rs], key="wu%d" % rs)
            wgv = wg[:, 0:8 * cw].rearrange("p (k n) -> p k n", n=cw)
            wuv = wu[:, 0:8 * cw].rearrange("p (k n) -> p k n", n=cw)
            for j in range(cw // 128):
                fc = ch * 4 + j
                gb, ub = fc % 2, 2 + fc % 2
                for kc in range(8):
                    S.op("pe", lambda h, kc=kc, j=j, gb=gb, wgv=wgv: h.matmul(bank(gb)[:, 0:T], lhsT=wgv[:, kc, j * 128:(j + 1) * 128], rhs=xnT[:, kc, 0:T], start=(kc == 0), stop=(kc == 7)),
                         reads=[xnT_key, "wg%d" % rs], writes=["bank%d" % gb])
                for kc in range(8):
                    S.op("pe", lambda h, kc=kc, j=j, ub=ub, wuv=wuv: h.matmul(bank(ub)[:, 0:T], lhsT=wuv[:, kc, j * 128:(j + 1) * 128], rhs=xnT[:, kc, 0:T], start=(kc == 0), stop=(kc == 7)),
                         reads=[xnT_key, "wu%d" % rs], writes=["bank%d" % ub])
                sgs = sg[fc % 2]
                S.op("act", lambda h, gb=gb, sgs=sgs: h.activation(out=sgs[:, 0:T], in_=bank(gb)[:, 0:T], func=AF.Silu), reads=["bank%d" % gb], writes=["sg%d" % (fc % 2)])
                S.op("dve", lambda h, ub=ub, sgs=sgs, fc=fc: h.tensor_tensor(out=hT[:, fc, 0:T], in0=bank(ub)[:, 0:T], in1=sgs[:, 0:T], op=ALU.mult),
                     reads=["bank%d" % ub, "sg%d" % (fc % 2)], writes=["hT%d" % fc])

    def ffn_down(nsub, hT, xt, base):
        for sub in range(nsub):
            for dh in range(2):
                yb = 4 + cnt["y"] % 2
                cnt["y"] += 1
                for fc in range(NFC):
                    S.op("pe", lambda h, fc=fc, sub=sub, dh=dh, yb=yb: h.matmul(bank(yb), lhsT=hT[:, fc, sub * 128:(sub + 1) * 128], rhs=wd_res[:, fc, dh * 512:(dh + 1) * 512], start=(fc == 0), stop=(fc == NFC - 1)),
                         reads=["hT%d" % fc, "wd%d" % (fc - fc % 2)], writes=["bank%d" % yb])
                xs_ = xt[:, base + sub, dh * 512:(dh + 1) * 512]
                S.op("dve", lambda h, yb=yb, xs_=xs_: h.scalar_tensor_tensor(out=xs_, in0=bank(yb), scalar=0.5, in1=xs_, op0=ALU.mult, op1=ALU.add),
                     reads=["bank%d" % yb, "xt%d" % (base + sub)], writes=["xt%d" % (base + sub)])

    try:
        items1, ditems1 = prep_items(0, w1g, w1u, w1d, "pool", ["dve", "act"], "pool")
        seq1 = PrepSeq(items1[:16] + ditems1[:4] + items1[16:] + ditems1[4:])
        dma("sp", gA, n1_d[0:1, :].partition_broadcast(128), writes=["gA"], key="gA")
        dma("sp", gB, nm_d[0:1, :].partition_broadcast(128), writes=["gB"], key="gB")

        wring = [(A.bf16([128, 4096]), A.bf16([128, 4096])) for _ in range(2)]
        xt = A.f32([128, 8, D])
        xn = [A.bf16([128, D]) for _ in range(2)]
        un = [A.bf16([128, D]) for _ in range(2)]
        xnT = A.bf16([128, 8, 512])
        uT = A.bf16([128, 8, 512])
        hT = A.bf16([128, NFC, 512])
        sg = [A.f32([128, 512]) for _ in range(2)]
        junk = [A.bf16([128, D]) for _ in range(3)]
        stt = [A.f32([128, 24]) for _ in range(4)]
        A1_PEAK = A.off

        xs_v = xs.rearrange("(t p) d -> p t d", p=128)
        h1_v = h1_d.rearrange("(t p) d -> p t d", p=128)

        def grp(g):
            nsub = 4 if g < NG - 1 else 1
            return nsub, nsub * 128, (g % 2) * 4

        def a1_load(g):
            nsub, T, base = grp(g)
            dma("sp", xt[:, base:base + nsub, :], xs_v[:, 4 * g:4 * g + nsub, :], writes=["xt%d" % (base + s) for s in range(nsub)], key="xtl%d" % (g % 2))

        def a1_norm(g, subs, ctx):
            nsub, T, base = grp(g)
            if "st" not in ctx:
                st = stt[cnt["st"] % 4]
                cnt["st"] += 1
                ctx["st"] = st
                ctx["rk"] = stats_rstd([xt[:, base + s, :] for s in range(nsub)], ["xt%d" % (base + s) for s in range(nsub)], D, st, ukey("st"), junk)
            st, rk = ctx["st"], ctx["rk"]
            for sub in subs:
                if sub >= nsub:
                    continue
                xb = xn[sub % 2]
                S.op("dve", lambda h, sub=sub, xb=xb, st=st, xt_=xt, base=base: h.scalar_tensor_tensor(out=xb, in0=xt_[:, base + sub, :], scalar=st[:, 16 + sub:17 + sub], in1=gA, op0=ALU.mult, op1=ALU.mult),
                     reads=["xt%d" % (base + sub), rk, "gA"], writes=["xn%d" % (sub % 2)])

        def a1_tr(g, subs):
            nsub, T, base = grp(g)
            for sub in subs:
                if sub >= nsub:
                    continue
                transposes_to(xn[sub % 2], "xn%d" % (sub % 2), xnT[:, :, sub * 128:(sub + 1) * 128], "xnT")

        def a1_pre(g):
            ctx = {}
            a1_load(g)
            for sub in range(4):
                a1_norm(g, [sub], ctx)
                a1_tr(g, [sub])

        def a1_post_a(g):
            nsub, T, base = grp(g)
            if g < NG_OWN:
                dma("pool", h1_v[:, 4 * g:4 * g + 4, :], xt[:, base:base + 4, :], reads=["xt%d" % (base + s) for s in range(4)], key="h1st%d" % (g % 2))
            st = stt[cnt["st"] % 4]
            cnt["st"] += 1
            skey = ukey("st")
            rk = stats_rstd([xt[:, base + s, :] for s in range(nsub)], ["xt%d" % (base + s) for s in range(nsub)], D, st, skey, junk)
            return st, rk

        def a1_post_b(g, st, rk):
            nsub, T, base = grp(g)
            for sub in range(nsub):
                ub_ = un[sub % 2]
                S.op("dve", lambda h, sub=sub, ub_=ub_, xt_=xt, base=base, st=st: h.scalar_tensor_tensor(out=ub_, in0=xt_[:, base + sub, :], scalar=st[:, 16 + sub:17 + sub], in1=gB, op0=ALU.mult, op1=ALU.mult),
                     reads=["xt%d" % (base + sub), rk, "gB"], writes=["un%d" % (sub % 2)])
                transposes_to(ub_, "un%d" % (sub % 2), uT[:, :, sub * 128:(sub + 1) * 128], "uT")
            dv = uT_d[g, :, 0:8 * T].rearrange("p (k t) -> p k t", t=T)
            dma("pool", dv, uT[:, :, 0:T], reads=["uT"], writes=[("uTd", g)], key="uTst")

        seq1.step(8)
        a1_pre(0)
        seq1.step(len(seq1.items))
        pend = None
        for g in range(NG):
            nsub, T, base = grp(g)
            ffn_gate_up(0, T, xnT, "xnT", hT, wring, [0], sg)
            if pend is not None:
                a1_post_b(*pend)
            ffn_gate_up(0, T, xnT, "xnT", hT, wring, [1, 2], sg)
            if g + 1 < NG:
                a1_load(g + 1)
            ffn_gate_up(0, T, xnT, "xnT", hT, wring, [3], sg)
            nctx = {}
            if g + 1 < NG:
                a1_norm(g + 1, [0, 1], nctx)
            ffn_gate_up(0, T, xnT, "xnT", hT, wring, [4, 5], sg)
            if g + 1 < NG:
                a1_tr(g + 1, [0])
                a1_norm(g + 1, [2], nctx)
                a1_tr(g + 1, [1])
                a1_norm(g + 1, [3], nctx)
                a1_tr(g + 1, [2, 3])
            ffn_down(nsub, hT, xt, base)
            st, rk = a1_post_a(g)
            pend = (g, st, rk)
        a1_post_b(*pend)

        if stop == "A1":
            raise _Stop()
        S.barrier()
        A.off = P_MARK
        w_in_sb = A.bf16([128, 8, 2304])
        uT2 = [A.bf16([128, 8, 512]) for _ in range(2)]
        cs_t = [(A.f32([128, 4, 32]), A.f32([128, 4, 32])) for _ in range(2)]
        rt = [A.f32([128, 256]) for _ in range(4)]
        ochunk = [A.bf16([128, 512]) for _ in range(3)]
        vst = [A.bf16([128, 4, 130]) for _ in range(2)]
        vwst = [A.bf16([128, 132]) for _ in range(2)]
        kst = [A.bf16([128, 4, 512]) for _ in range(2)]
        qst = [A.bf16([128, 4, 512]) for _ in range(2)]
        kwst = [A.bf16([128, 512]) for _ in range(2)]
        qwst = [A.bf16([128, 4, 512]) for _ in range(2)]
        A2_PEAK = A.off

        w_in_v = w_in_d.rearrange("(k p) n -> p k n", p=128)
        for kc in range(8):
            for hh in range(2):
                dma("pool", w_in_sb[:, kc, hh * 1152:(hh + 1) * 1152], w_in_v[:, kc, hh * 1152:(hh + 1) * 1152], writes=[("w_in", kc, hh)], key="w_in%d_%d" % (kc, hh))
        for s in range(2):
            S.op("pool", lambda h, s=s: h.memset(vst[s][:, :, 128:130], 1.0), writes=["vst%d" % s])
            S.op("pool", lambda h, s=s: h.memset(vwst[s], 1.0), writes=["vwst%d" % s])

        cos_v = cos_d.rearrange("(t p) f -> p t f", p=128)
        sin_v = sin_d.rearrange("(t p) f -> p t f", p=128)
        c2 = {"z": 0, "rt": 0, "oc": 0, "tp": 0, "v": 0}
        WIN_KEYS = ["w_in%d" % kc for kc in range(8)]

        def zproj(uTt, uT_key, sub, c0, cw):
            zb = c2["z"] % 5
            c2["z"] += 1
            for kc in range(8):
                S.op("pe", lambda h, kc=kc, zb=zb: h.matmul(bank(zb)[:, 0:cw], lhsT=uTt[:, kc, sub * 128:(sub + 1) * 128], rhs=w_in_sb[:, kc, c0:c0 + cw], start=(kc == 0), stop=(kc == 7)),
                     reads=[uT_key, ("w_in", kc, 0), ("w_in", kc, 1)], writes=["bank%d" % zb])
            return zb

        def rope(zb, U, cs, cs_keys, sub):
            oc_i = c2["oc"] % 3
            c2["oc"] += 1
            oc = ochunk[oc_i]
            okey = "oc%d" % oc_i
            zv = bank(zb)[:, 0:U * 64].rearrange("p (u two f) -> p u two f", two=2, f=32)
            ov = oc[:, 0:U * 64].rearrange("p (u two f) -> p u two f", two=2, f=32)
            cb = cs[0][:, sub, :].unsqueeze(1).broadcast_to([128, U, 32])
            sb_ = cs[1][:, sub, :].unsqueeze(1).broadcast_to([128, U, 32])
            for half in range(2):
                ta = rt[c2["rt"] % 4]; ka = "rt%d" % (c2["rt"] % 4); c2["rt"] += 1
                tb_ = rt[c2["rt"] % 4]; kb = "rt%d" % (c2["rt"] % 4); c2["rt"] += 1
                tav = ta[:, 0:U * 32].rearrange("p (u f) -> p u f", f=32)
                tbv = tb_[:, 0:U * 32].rearrange("p (u f) -> p u f", f=32)
                S.op("dve", lambda h, tav=tav, half=half: h.tensor_tensor(out=tav, in0=zv[:, :, half, :], in1=cb, op=ALU.mult), reads=["bank%d" % zb] + cs_keys, writes=[ka])
                S.op("dve", lambda h, tbv=tbv, half=half: h.tensor_tensor(out=tbv, in0=zv[:, :, 1 - half, :], in1=sb_, op=ALU.mult), reads=["bank%d" % zb] + cs_keys, writes=[kb])
                op = ALU.subtract if half == 0 else ALU.add
                S.op("dve", lambda h, tav=tav, tbv=tbv, half=half, op=op: h.tensor_tensor(out=ov[:, :, half, :], in0=tav, in1=tbv, op=op), reads=[ka, kb], writes=[okey + "_%d" % half])
            return oc, [okey + "_0", okey + "_1"]

        def tr_blocks(src, src_keys, nblk, dst, dst_key):
            tb = 5 + c2["tp"] % 3
            c2["tp"] += 1
            tpv = bank_bf(tb)
            for k in range(nblk):
                S.op("pe", lambda h, k=k: h.transpose(out=tpv[:, k, :], in_=src[:, k * 128:(k + 1) * 128], identity=ident), reads=list(src_keys) + ["ident"], writes=["bank%d" % tb])
            S.op("act", lambda h: h.activation(out=dst, in_=tpv[:, 0:nblk, :], func=AF.Copy), reads=["bank%d" % tb], writes=[dst_key])

        pendq = []

        def defer(fn):
            pendq.append(fn)
            while len(pendq) > 2:
                pendq.pop(0)()

        def flush():
            while pendq:
                pendq.pop(0)()

        def bq_post(oc, ok, qws_, sub, us):
            tb = 5 + c2["tp"] % 3
            c2["tp"] += 1
            tpv = bank_bf(tb)
            ocv = oc.rearrange("p (g hq d) -> p g hq d", g=2, hq=4)
            for hq in range(4):
                for g2 in range(2):
                    S.op("pe", lambda h, hq=hq, g2=g2: h.transpose(out=tpv[g2 * 64:(g2 + 1) * 64, hq, :], in_=ocv[:, g2, hq, :], identity=ident),
                         reads=list(ok) + ["ident"], writes=["bank%d" % tb])
            S.op("act", lambda h: h.activation(out=qws_[:, sub, :].rearrange("p (hq q) -> p hq q", q=128), in_=tpv[:, 0:4, :], func=AF.Copy),
                 reads=["bank%d" % tb], writes=["qwst%d" % us])

        for g in range(NG):
            nsub, T, _ = grp(g)
            own = g < NG_OWN
            us = g % 2
            uTt = uT2[us]
            dma("sp", uTt[:, :, 0:T], uT_d[g, :, 0:8 * T].rearrange("p (k t) -> p k t", t=T), reads=[("uTd", g)], writes=["uT2_%d" % us], key="uT2_%d" % us)
            cs = cs_t[us]
            dma("sp", cs[0][:, 0:nsub, :], cos_v[:, 4 * g:4 * g + nsub, :], writes=["cos%d" % us], key="cos%d" % us)
            dma("sp", cs[1][:, 0:nsub, :], sin_v[:, 4 * g:4 * g + nsub, :], writes=["sin%d" % us], key="sin%d" % us)
            cs_keys = ["cos%d" % us, "sin%d" % us]
            ks_, qs_, kws_, qws_ = kst[us], qst[us], kwst[us], qwst[us]
            for sub in range(nsub):
                t = 4 * g + sub
                vs_i = c2["v"] % 2
                c2["v"] += 1
                vs_ = vst[vs_i]
                zb = zproj(uTt, "uT2_%d" % us, sub, 1024, 512)
                S.op("act", lambda h, zb=zb, vs_=vs_: h.activation(out=vs_[:, :, 0:128], in_=bank(zb).rearrange("p (h e) -> p h e", e=128), func=AF.Copy),
                     reads=["bank%d" % zb], writes=["vst%d" % vs_i])
                dma("pool", VA_d[:, :, t, :].rearrange("h p e -> p h e"), vs_, reads=["vst%d" % vs_i], writes=[("VAd", t)], key="vast%d" % vs_i)
                zb = zproj(uTt, "uT2_%d" % us, sub, 2048, 256)
                vw_ = vwst[vs_i]
                S.op("act", lambda h, zb=zb, vw_=vw_: h.activation(out=vw_.rearrange("p (g e) -> p g e", e=66)[:, :, 0:64], in_=bank(zb)[:, 128:256].rearrange("p (g e) -> p g e", e=64), func=AF.Copy),
                     reads=["bank%d" % zb], writes=["vwst%d" % vs_i, "bank%d" % zb])
                dma("pool", VW_d[:, t, :], vw_, reads=["vwst%d" % vs_i], writes=[("VWd", t)], key="vwst%d" % vs_i)
                oc, ok = rope(zb, 2, cs, cs_keys, sub)
                defer(lambda oc=oc, ok=ok, sub=sub, kws_=kws_, us=us: tr_blocks(oc, ok, 1, kws_[:, sub * 128:(sub + 1) * 128].rearrange("p (k t) -> p k t", k=1), "kwst%d" % us))
                zb = zproj(uTt, "uT2_%d" % us, sub, 512, 512)
                oc, ok = rope(zb, 8, cs, cs_keys, sub)
                defer(lambda oc=oc, ok=ok, sub=sub, ks_=ks_, us=us: tr_blocks(oc, ok, 4, ks_[:, :, sub * 128:(sub + 1) * 128], "kst%d" % us))
                if own:
                    zb = zproj(uTt, "uT2_%d" % us, sub, 0, 512)
                    oc, ok = rope(zb, 8, cs, cs_keys, sub)
                    defer(lambda oc=oc, ok=ok, sub=sub, qs_=qs_, us=us: tr_blocks(oc, ok, 4, qs_[:, :, sub * 128:(sub + 1) * 128], "qst%d" % us))
                    zb = zproj(uTt, "uT2_%d" % us, sub, 1536, 512)
                    oc, ok = rope(zb, 8, cs, cs_keys, sub)
                    defer(lambda oc=oc, ok=ok, sub=sub, qws_=qws_, us=us: bq_post(oc, ok, qws_, sub, us))
            flush()
            c0t = 512 * g if g < NG - 1 else META_T * 128
            dma("pool", KTA_d[:, :, c0t:c0t + T].rearrange("h p t -> p h t"), ks_[:, :, 0:T], reads=["kst%d" % us], writes=[("KTAd", g)], key="kstst%d" % us)
            dma("pool", KW_d[:, c0t:c0t + T], kws_[:, 0:T], reads=["kwst%d" % us], writes=[("KWd", g)], key="kwstst%d" % us)
            if own:
                dma("pool", QTA_d[:, :, 512 * g:512 * g + 512].rearrange("h p t -> p h t"), qs_, reads=["qst%d" % us], writes=[("QTAd", g)], key="qstst%d" % us)
                dma("pool", QW_d[:, 4 * g:4 * g + 4, :], qws_, reads=["qwst%d" % us], writes=[("QWd", g)], key="qwstst%d" % us)

        if stop == "A2":
            raise _Stop()
        S.barrier()
        A.off = P_MARK
        KVr = [(A.bf16([128, NTOK]), A.bf16([128, NT * 132])) for _ in range(2)]
        QTt = [A.bf16([128, 512]) for _ in range(2)]
        PT = [A.bf16([128, 2, 512]) for _ in range(3)]
        osb = [A.f32([128, 8, 129]) for _ in range(2)]
        otmp = [A.f32([128, 128]) for _ in range(2)]
        ofin = [A.f32([128, 128]) for _ in range(2)]
        cst = [A.bf16([128, 4, 128]) for _ in range(2)]
        est = [A.f32([128, 40]) for _ in range(2)]
        owsb = A.f32([128, 8, 65])
        owf = A.f32([128, 512])
        cwst = [A.bf16([128, 512]) for _ in range(2)]
        junkb = A.f32([128, 512])
        PTw = [A.bf16([128, 512]) for _ in range(16)]
        B_PEAK = A.off

        if debug:
            dbg_v = nc.dram_tensor("dbg_v", [4, 128, NT * 130], BF16, kind="ExternalOutput").ap()
            dbg_osb = nc.dram_tensor("dbg_osb", [4 * (HALF // 512), 128, 8, 129], F32, kind="ExternalOutput").ap()
        items2, ditems2 = prep_items(1, w2g, w2u, w2d, "pool", ["pool"], "pool")
        prep2 = ditems2 + items2
        seq2 = PrepSeq(prep2)

        def prep2_some(n):
            seq2.step(n)

        all_KT_keys = [("KTAd", g) for g in range(NG)]
        all_VA_keys = [("VAd", t) for t in range(NT)]
        all_KW_keys = [("KWd", g) for g in range(NG)]
        all_VW_keys = [("VWd", t) for t in range(NT)]
        cb = {"s": 0, "pt": 0, "u": 0}

        def acc_ap(idx):
            b = 4 + idx // 3
            col = (idx % 3) * 170
            return bank(b)[:, col:col + 129], b

        NQT = HALF // 512
        cat_v = cat_d.rearrange("(t p) d -> p t d", p=128)
        steps = [(hd, qt, kt) for hd in range(4) for qt in range(NQT) for kt in range(NT)]

        def load_kv(hd):
            s = hd % 2
            K_, V_ = KVr[s]
            dma("sp", K_, KTA_d[hd], reads=all_KT_keys, writes=["KTs%d" % s], key="KTl%d" % s)
            dma("sp", V_[:, 0:NT * 130], VA_d[hd].rearrange("p t e -> p (t e)"), reads=all_VA_keys, writes=["Vs%d" % s], key="Vl%d" % s)
            if debug:
                dma("pool", dbg_v[hd], V_[:, 0:NT * 130], reads=["Vs%d" % s], key="dbgv%d" % s)

        def load_q(hd, qt):
            i = (hd * NQT + qt) % 2
            dma("sp", QTt[i], QTA_d[hd, :, qt * 512:(qt + 1) * 512], reads=[("QTAd", qt)], writes=["QT%d" % i], key="QTl%d" % i)

        def rec_S(step):
            hd, qt, kt = step
            s = hd % 2
            K_ = KVr[s][0]
            qi = (hd * NQT + qt) % 2
            kk = 128 if kt < META_T else N_META
            sb_i = cb["s"] % 2
            cb["s"] += 1
            for c in range(2):
                b = 2 * sb_i + c
                S.op("pe", lambda h, c=c, b=b, kt=kt, kk=kk, K_=K_, qi=qi: h.matmul(bank(b)[0:kk, :], lhsT=K_[c * 64:(c + 1) * 64, kt * 128:kt * 128 + kk], rhs=QTt[qi][c * 64:(c + 1) * 64, :], start=True, stop=True),
                     reads=["KTs%d" % s, "QT%d" % qi], writes=["sbank%d" % sb_i])
            pi = cb["pt"] % 3
            cb["pt"] += 1
            S.op("act", lambda h, sb_i=sb_i, pi=pi, kk=kk: h.activation(out=PT[pi][0:kk].rearrange("p c q -> p (c q)"), in_=ps[0:kk, sb_i * 1024:(sb_i + 1) * 1024], func=AF.Exp, scale=0.125),
                 reads=["sbank%d" % sb_i], writes=["PT%d" % pi])
            return pi

        def rec_PV(step, pi):
            hd, qt, kt = step
            s = hd % 2
            V_ = KVr[s][1][:, 0:NT * 130].rearrange("p (t e) -> p t e", e=130)
            kk = 128 if kt < META_T else N_META
            for c in range(2):
                for qc in range(4):
                    idx = c * 4 + qc
                    oap, b = acc_ap(idx)
                    first = (kt == 0 and idx % 3 == 0)
                    S.op("pe", lambda h, c=c, qc=qc, oap=oap, first=first, kk=kk, kt=kt, V_=V_, pi=pi: h.matmul(oap, lhsT=PT[pi][0:kk, c, qc * 128:(qc + 1) * 128], rhs=V_[0:kk, kt, 0:129], start=first, stop=(kt == NT - 1), skip_group_check=True),
                         reads=["PT%d" % pi, "Vs%d" % s], writes=["oacc"])

        def rec_epilogue(hd, qt):
            u = cb["u"] % 2
            cb["u"] += 1
            ob = osb[u]
            e = est[u]
            for idx in range(8):
                oap, b = acc_ap(idx)
                S.op("dve", lambda h, idx=idx, oap=oap: h.tensor_copy(out=ob[:, idx, :], in_=oap), reads=["oacc"], writes=["osb%d_%d" % (u, idx)])
            okeys = ["osb%d_%d" % (u, i) for i in range(8)]
            if debug:
                dma("pool", dbg_osb[hd * NQT + qt], ob, reads=okeys, key="dbgosb%d" % u)
            S.op("dve", lambda h: h.reciprocal(out=e[:, 0:8], in_=ob[:, :, 128]), reads=okeys, writes=["est%d_r" % u])
            S.op("dve", lambda h: h.tensor_scalar(out=e[:, 8:12], in0=e[:, 4:8], scalar1=neg_lam, scalar2=None, op0=ALU.mult), reads=["est%d_r" % u, "neg_lam"], writes=["est%d_rl" % u])
            for qc in range(4):
                ot = otmp[qc % 2]
                of = ofin[qc % 2]
                S.op("dve", lambda h, qc=qc, ot=ot: h.tensor_scalar(out=ot, in0=ob[:, qc, 0:128], scalar1=e[:, qc:qc + 1], scalar2=None, op0=ALU.mult),
                     reads=okeys + ["est%d_r" % u], writes=["otmp%d" % (qc % 2)])
                S.op("dve", lambda h, qc=qc, ot=ot, of=of: h.scalar_tensor_tensor(out=of, in0=ob[:, 4 + qc, 0:128], scalar=e[:, 8 + qc:9 + qc], in1=ot, op0=ALU.mult, op1=ALU.add),
                     reads=okeys + ["est%d_rl" % u, "otmp%d" % (qc % 2)], writes=["ofin%d" % (qc % 2)])
                S.op("dve", lambda h, qc=qc, of=of: h.scalar_tensor_tensor(out=junkb[:, qc * 128:(qc + 1) * 128], in0=of, scalar=1.0, in1=of, op0=ALU.mult, op1=ALU.mult, accum_out=e[:, 16 + qc:17 + qc]),
                     reads=["ofin%d" % (qc % 2)], writes=["est%d_ss%d" % (u, qc), "junkb%d" % qc])
                S.op("dve", lambda h, qc=qc: h.tensor_scalar(out=e[:, 24 + qc:25 + qc], in0=e[:, 16 + qc:17 + qc], scalar1=1.0 / 128, scalar2=EPS, op0=ALU.mult, op1=ALU.add),
                     reads=["est%d_ss%d" % (u, qc)], writes=["est%d_v%d" % (u, qc)])
                S.op("pool", lambda h, qc=qc: h.tensor_tensor(out=e[:, 32 + qc:33 + qc], in0=e[:, 24 + qc:25 + qc], in1=mhalf[:, 0:1], op=ALU.pow),
                     reads=["est%d_v%d" % (u, qc), "mhalf"], writes=["est%d_rs%d" % (u, qc)])
                S.op("dve", lambda h, qc=qc, of=of: h.scalar_tensor_tensor(out=cst[u][:, qc, :], in0=of, scalar=e[:, 32 + qc:33 + qc], in1=dgain, op0=ALU.mult, op1=ALU.mult),
                     reads=["ofin%d" % (qc % 2), "est%d_rs%d" % (u, qc), "dgain"], writes=["cst%d_%d" % (u, qc)])
            dma("pool", cat_v[:, qt * 4:qt * 4 + 4, hd * 128:(hd + 1) * 128], cst[u], reads=["cst%d_%d" % (u, qc) for qc in range(4)], writes=[("catd", qt)], key="cstst%d" % u)

        n_prep_per_unit = (len(prep2) + 4 * NQT - 1) // (4 * NQT) + 1
        load_kv(0)
        load_q(0, 0)
        pend_pv = None
        for i, step in enumerate(steps):
            hd, qt, kt = step
            pi = rec_S(step)
            if pend_pv is not None:
                pstep, ppi = pend_pv
                rec_PV(pstep, ppi)
                if pstep[2] == NT - 1:
                    rec_epilogue(pstep[0], pstep[1])
                    prep2_some(n_prep_per_unit)
            if kt == 0:
                nxt = hd * NQT + qt + 1
                if nxt < 4 * NQT:
                    nh, nq = divmod(nxt, NQT)
                    load_q(nh, nq)
                if qt == 0 and hd + 1 < 4:
                    load_kv(hd + 1)
            pend_pv = (step, pi)
        rec_PV(*pend_pv)
        rec_epilogue(pend_pv[0][0], pend_pv[0][1])
        prep2_some(len(prep2))

        if stop == "Bd":
            raise _Stop()
        Kw_, Vw_ = KVr[0]
        dma("sp", Kw_, KW_d, reads=all_KW_keys, writes=["KTs0"], key="KTl0")
        dma("sp", Vw_, VW_d.rearrange("p t e -> p (t e)"), reads=all_VW_keys, writes=["Vs0"], key="Vl0")
        Vwv = Vw_.rearrange("p (t g e) -> p t g e", g=2, e=66)
        S.barrier()
        wc = {"pt": 0, "s": 0}

        def w_front(n):
            qi = n % 2
            dma("sp", QTt[qi], QW_d[:, n, :], reads=[("QWd", n // 4)], writes=["QT%d" % qi], key="QTl%d" % qi)
            left = n - 1 if n > 0 else NT - 2
            right = n + 1
            blocks = [(left, 0 if n > 0 else 2, 128), (n, None, 128), (right, 1 if n < NOWN_T - 1 else 3, 128), (META_T, None, N_META)]
            pts = []
            for bi, (tix, mi, kk) in enumerate(blocks):
                for g2 in range(2):
                    sbk = wc["s"] % 4
                    wc["s"] += 1
                    S.op("pe", lambda h, g2=g2, sbk=sbk, tix=tix, kk=kk, qi=qi: h.matmul(bank(sbk)[0:kk, :], lhsT=Kw_[g2 * 64:(g2 + 1) * 64, tix * 128:tix * 128 + kk], rhs=QTt[qi][g2 * 64:(g2 + 1) * 64, :], start=True, stop=True),
                         reads=["KTs0", "QT%d" % qi], writes=["wsb%d" % sbk])
                    pslot = wc["pt"] % 16
                    wc["pt"] += 1
                    pap = PTw[pslot]
                    pkey = "PTw%d" % pslot
                    S.op("act", lambda h, sbk=sbk, pap=pap, kk=kk: h.activation(out=pap[0:kk, :], in_=bank(sbk)[0:kk, :], func=AF.Exp, scale=0.125),
                         reads=["wsb%d" % sbk], writes=[pkey])
                    if mi is not None:
                        S.op("dve", lambda h, pap=pap, mi=mi: h.tensor_tensor(out=pap, in0=pap, in1=masks[:, mi, :], op=ALU.mult), reads=[pkey, "mask%d" % mi], writes=[pkey])
                    pts.append((pap, pkey, g2, tix, kk, bi))
            return pts

        def w_back(n, pts):
            for (pap, pkey, g2, tix, kk, bi) in pts:
                for hq in range(4):
                    S.op("pe", lambda h, pap=pap, g2=g2, tix=tix, kk=kk, bi=bi, hq=hq: h.matmul(bank(4 + g2)[:, hq * 128:hq * 128 + 65], lhsT=pap[0:kk, hq * 128:(hq + 1) * 128], rhs=Vwv[0:kk, tix, g2, 0:65], start=(bi == 0 and hq == 0), stop=(bi == 3), skip_group_check=True),
                         reads=[pkey, "Vs0"], writes=["woacc%d" % g2])
            for g2 in range(2):
                S.op("dve", lambda h, g2=g2: h.tensor_copy(out=owsb[:, g2 * 4:(g2 + 1) * 4, :], in_=bank(4 + g2).rearrange("p (hq e) -> p hq e", e=128)[:, :, 0:65]),
                     reads=["woacc%d" % g2], writes=["owsb%d" % g2])
            e = est[n % 2]
            S.op("dve", lambda h, e=e: h.tensor_tensor(out=e[:, 0:8], in0=owsb[:, :, 64], in1=esink, op=ALU.add), reads=["owsb0", "owsb1", "esink"], writes=["west_d"])
            S.op("dve", lambda h, e=e: h.reciprocal(out=e[:, 8:16], in_=e[:, 0:8]), reads=["west_d"], writes=["west_r"])
            S.op("dve", lambda h, e=e: h.tensor_tensor(out=owf.rearrange("p (hh d) -> p hh d", d=64), in0=owsb[:, :, 0:64], in1=e[:, 8:16].unsqueeze(2).broadcast_to([128, 8, 64]), op=ALU.mult),
                 reads=["owsb0", "owsb1", "west_r"], writes=["owf"])
            S.op("dve", lambda h, e=e: h.scalar_tensor_tensor(out=junkb, in0=owf, scalar=1.0, in1=owf, op0=ALU.mult, op1=ALU.mult, accum_out=e[:, 16:17]), reads=["owf"], writes=["west_ss"] + ["junkb%d" % i for i in range(4)])
            S.op("dve", lambda h, e=e: h.tensor_scalar(out=e[:, 17:18], in0=e[:, 16:17], scalar1=1.0 / 512, scalar2=EPS, op0=ALU.mult, op1=ALU.add), reads=["west_ss"], writes=["west_v"])
            S.op("pool", lambda h, e=e: h.tensor_tensor(out=e[:, 18:19], in0=e[:, 17:18], in1=mhalf[:, 0:1], op=ALU.pow), reads=["west_v", "mhalf"], writes=["west_rs"])
            cw_ = cwst[n % 2]
            S.op("dve", lambda h, e=e, cw_=cw_: h.scalar_tensor_tensor(out=cw_, in0=owf, scalar=e[:, 18:19], in1=wgain, op0=ALU.mult, op1=ALU.mult), reads=["owf", "west_rs", "wgain"], writes=["cwst%d" % (n % 2)])
            dma("pool", cat_v[:, n, 512:1024], cw_, reads=["cwst%d" % (n % 2)], writes=[("catd_w", n)], key="cwstst%d" % (n % 2))

        pts_next = w_front(0)
        for n in range(NOWN_T):
            pts_cur = pts_next
            if n + 1 < NOWN_T:
                pts_next = w_front(n + 1)
            w_back(n, pts_cur)

        if stop == "B":
            raise _Stop()
        S.barrier()
        A.off = P_MARK - (2 * 2048 + 2 * 1024)
        dma("sp", gA, n2_d[0:1, :].partition_broadcast(128), writes=["gA"], key="gA")
        dma("sp", gB, nf_d[0:1, :].partition_broadcast(128), writes=["gB"], key="gB")
        wringC = [(A.bf16([128, 4096]), A.bf16([128, 4096])) for _ in range(2)]
        w_out_sb = A.bf16([128, 8, D])
        xtC = A.f32([128, 8, D])
        catt = A.bf16([128, 4, D])
        catT = A.bf16([128, 8, 512])
        xnC = [A.bf16([128, D]) for _ in range(2)]
        xnTC = A.bf16([128, 8, 512])
        hTC = A.bf16([128, NFC, 512])
        sgC = [A.f32([128, 512]) for _ in range(2)]
        junkC = [A.bf16([128, D]) for _ in range(3)]
        sttC = [A.f32([128, 24]) for _ in range(4)]
        C_PEAK = A.off

        w_out_v = w_out_d.rearrange("(k p) n -> p k n", p=128)
        for kc in range(8):
            dma("pool", w_out_sb[:, kc, :], w_out_v[:, kc, :], writes=["w_out%d" % kc], key="w_out%d" % kc)
        y_v = y_d.rearrange("(t p) d -> p t d", p=128)

        def c_pre(g):
            base = (g % 2) * 4
            dma("sp", xtC[:, base:base + 4, :], h1_v[:, 4 * g:4 * g + 4, :], writes=["xt%d" % (base + s) for s in range(4)], key="xtl%d" % (g % 2))
            dma("sp", catt, cat_v[:, 4 * g:4 * g + 4, :], reads=[("catd", g)] + [("catd_w", 4 * g + s) for s in range(4)], writes=["catt"], key="cattl")
            for sub in range(4):
                transposes_to(catt[:, sub, :], "catt", catT[:, :, sub * 128:(sub + 1) * 128], "catT")
            for sub in range(4):
                for dh in range(2):
                    yb = 4 + cnt["y"] % 2
                    cnt["y"] += 1
                    for kc in range(8):
                        S.op("pe", lambda h, kc=kc, sub=sub, dh=dh, yb=yb: h.matmul(bank(yb), lhsT=catT[:, kc, sub * 128:(sub + 1) * 128], rhs=w_out_sb[:, kc, dh * 512:(dh + 1) * 512], start=(kc == 0), stop=(kc == 7)),
                             reads=["catT", "w_out%d" % kc], writes=["bank%d" % yb])
                    xs_ = xtC[:, base + sub, dh * 512:(dh + 1) * 512]
                    S.op("dve", lambda h, yb=yb, xs_=xs_: h.tensor_tensor(out=xs_, in0=bank(yb), in1=xs_, op=ALU.add), reads=["bank%d" % yb, "xt%d" % (base + sub)], writes=["xt%d" % (base + sub)])
            st = sttC[cnt["st"] % 4]
            cnt["st"] += 1
            rk = stats_rstd([xtC[:, base + s, :] for s in range(4)], ["xt%d" % (base + s) for s in range(4)], D, st, ukey("st"), junkC)
            for sub in range(4):
                xb = xnC[sub % 2]
                S.op("dve", lambda h, sub=sub, xb=xb, st=st: h.scalar_tensor_tensor(out=xb, in0=xtC[:, base + sub, :], scalar=st[:, 16 + sub:17 + sub], in1=gA, op0=ALU.mult, op1=ALU.mult),
                     reads=["xt%d" % (base + sub), rk, "gA"], writes=["xn%d" % (sub % 2)])
                transposes_to(xb, "xn%d" % (sub % 2), xnTC[:, :, sub * 128:(sub + 1) * 128], "xnT")

        def c_post(g):
            base = (g % 2) * 4
            st = sttC[cnt["st"] % 4]
            cnt["st"] += 1
            rk = stats_rstd([xtC[:, base + s, :] for s in range(4)], ["xt%d" % (base + s) for s in range(4)], D, st, ukey("st"), junkC)
            for sub in range(4):
                S.op("dve", lambda h, sub=sub, st=st: h.scalar_tensor_tensor(out=xtC[:, base + sub, :], in0=xtC[:, base + sub, :], scalar=st[:, 16 + sub:17 + sub], in1=gB, op0=ALU.mult, op1=ALU.mult),
                     reads=["xt%d" % (base + sub), rk, "gB"], writes=["xt%d" % (base + sub)])
            return dma("pool", y_v[:, 4 * g:4 * g + 4, :], xtC[:, base:base + 4, :], reads=["xt%d" % (base + s) for s in range(4)], key="yst%d" % (g % 2))

        c_pre(0)
        out_toks = []
        for g in range(NG_OWN):
            base = (g % 2) * 4
            ffn_gate_up(1, 512, xnTC, "xnT", hTC, wringC, [0, 1, 2, 3, 4, 5], sgC)
            if g + 1 < NG_OWN:
                c_pre(g + 1)
            ffn_down(4, hTC, xtC, base)
            out_toks.append(c_post(g))
        S.op("pool", lambda h: h.nop(), extra_deps=out_toks[-2:])
    except _Stop:
        S.barrier()

    print("[kernel] arena peak (words): %d of %d" % (A.peak, ARENA_W))
    print("[kernel] instr counts:", {e: len(S.streams[e]) for e in ENGS})
    S.finalize()
    sems = {e: es.enter_context(nc.semaphore("sem_%s" % e)) for e in ENGS}
    dsems = {k: es.enter_context(nc.semaphore("dsem_%s" % k)) for k in S.dma_cnt}
    print("[kernel] semaphores:", len(sems) + len(dsems))
    with nc.Block() as block:
        @block.tensor
        def _(h):
            S.emit_engine("pe", h, sems, dsems)

        @block.scalar
        def _(h):
            S.emit_engine("act", h, sems, dsems)

        @block.vector
        def _(h):
            S.emit_engine("dve", h, sems, dsems)

        @block.gpsimd
        def _(h):
            S.emit_engine("pool", h, sems, dsems)

        @block.sync
        def _(h):
            S.emit_engine("sp", h, sems, dsems)
    es.close()
    return nc


def _rope_tables(pos):
    inv = np.float32(10000.0) ** (-(np.arange(0, 64, 2, dtype=np.float32)) / np.float32(64))
    ang = (pos.astype(np.float32)[:, None] * inv[None, :]).astype(np.float32)
    return np.cos(ang).astype(np.float32), np.sin(ang).astype(np.float32)


def kernel(x, meta_tokens, ffn1_norm, ffn1_w_gate, ffn1_w_up, ffn1_w_down, mix_norm, w_in,
           lambda_q1, lambda_k1, lambda_q2, lambda_k2, diff_norm, win_sink, win_norm, w_out,
           ffn2_norm, ffn2_w_gate, ffn2_w_up, ffn2_w_down, final_norm):
    x = np.asarray(x, dtype=np.float32)
    B, SEQ, _ = x.shape
    HALF = SEQ // 2
    NT = SEQ // 128 + 1
    NTOK = NT * 128
    nc = build_program(SEQ)
    f32 = lambda a: np.ascontiguousarray(np.asarray(a, dtype=np.float32))
    ident = np.eye(128, dtype=np.float32).astype(bf)
    jj = np.arange(128)[:, None]
    qq = np.arange(128)[None, :]
    mL = np.tile((jj >= qq).astype(np.float32), (1, 4)).astype(bf)
    mR = np.tile((jj <= qq).astype(np.float32), (1, 4)).astype(bf)
    mZ = np.zeros((128, 512), dtype=bf)
    meta_pad = np.zeros((128, 1024), np.float32)
    meta_pad[:N_META] = np.asarray(meta_tokens, np.float32)
    common = {
        "ident": ident, "maskL": mL, "maskR": mR,
        "w1g": f32(ffn1_w_gate[0]), "w1u": f32(ffn1_w_up[0]), "w1d": f32(ffn1_w_down[0]),
        "w2g": f32(ffn2_w_gate[0]), "w2u": f32(ffn2_w_up[0]), "w2d": f32(ffn2_w_down[0]),
        "w_in": f32(w_in[0]), "w_out": f32(w_out[0]),
        "n1": f32(ffn1_norm).reshape(1, -1), "nm": f32(mix_norm).reshape(1, -1), "n2": f32(ffn2_norm).reshape(1, -1), "nf": f32(final_norm).reshape(1, -1),
        "lq1": f32(lambda_q1).reshape(1, -1), "lk1": f32(lambda_k1).reshape(1, -1), "lq2": f32(lambda_q2).reshape(1, -1), "lk2": f32(lambda_k2).reshape(1, -1),
        "dnorm": f32(diff_norm).reshape(1, -1), "sink": f32(win_sink).reshape(1, -1), "wnorm": f32(win_norm).reshape(1, -1),
    }
    in_maps = []
    for c in range(8):
        b, half = divmod(c, 2)
        own = slice(half * HALF, (half + 1) * HALF)
        oth = slice((1 - half) * HALF, (2 - half) * HALF)
        xs = np.concatenate([x[b, own], x[b, oth], meta_pad], axis=0)
        pos = np.concatenate([N_META + np.arange(SEQ)[own], N_META + np.arange(SEQ)[oth], np.arange(N_META), np.zeros(128 - N_META, np.int64)])
        cs, sn = _rope_tables(pos)
        m = dict(common)
        m["xs"] = np.ascontiguousarray(xs)
        m["cosr"] = cs
        m["sinr"] = sn
        m["maskLe"] = mZ if half == 0 else mL
        m["maskRe"] = mR if half == 0 else mZ
        in_maps.append(m)
    res = run_bass_kernel_spmd(nc, in_maps, core_ids=list(range(8)))
    out = np.empty((B, SEQ, 1024), np.float32)
    for c in range(8):
        b, half = divmod(c, 2)
        out[b, half * HALF:(half + 1) * HALF] = res.results[c]["y"]
    return out
```
